# Optimizing a Trainium2 kernel written in Bass

```python
import math
import jax
import jax.numpy as jnp
from jax import lax
import numpy as np

D_MODEL = 1024
BATCH = 8
SEQ = 4096
DEPTH = 4

GRID_W = 64
CTX_LEN = 256
EPS = 1e-6
N_MOD = 9
D_FF = 2816

SSD_EXPAND = 2
D_INNER = SSD_EXPAND * D_MODEL
SSD_HEAD_DIM = 64
SSD_HEADS = D_INNER // SSD_HEAD_DIM
SSD_GROUPS = 4
SSD_HEADS_PER_GROUP = SSD_HEADS // SSD_GROUPS
D_STATE = 128
D_CONV = 3
SSD_CHUNK = 128
BC_DIM = SSD_GROUPS * D_STATE
CONV_DIM = D_INNER + 2 * BC_DIM

ATT_HEAD_DIM = 128
N_Q_HEADS = D_MODEL // ATT_HEAD_DIM
N_KV_HEADS = 2
Q_PER_KV = N_Q_HEADS // N_KV_HEADS
ATT_DIM = N_Q_HEADS * ATT_HEAD_DIM
KV_DIM = N_KV_HEADS * ATT_HEAD_DIM
ATT_SCALE = ATT_HEAD_DIM ** -0.5
Q_BLOCK = 128
ROPE_THETA = 10000.0
ROPE_AXIS_DIM = ATT_HEAD_DIM // 2

IN_SPLITS = (D_INNER, CONV_DIM, SSD_HEADS, SSD_HEADS, ATT_DIM, KV_DIM, KV_DIM, D_MODEL, D_MODEL)
IN_DIM = sum(IN_SPLITS)

kernel_name = "hybrid_ssd_gqa_prefix_dit_block"


def split_cols(u, sizes):
    out, start = [], 0
    for s in sizes:
        out.append(u[..., start:start + s])
        start += s
    return out


def rms_norm(x, g):
    xf = x.astype(jnp.float32)
    xf = xf * lax.rsqrt(jnp.mean(xf * xf, axis=-1, keepdims=True) + EPS)
    return (xf * g.astype(jnp.float32)).astype(x.dtype)


def group_rms_norm(y, g, n_groups):
    yf = y.astype(jnp.float32).reshape(*y.shape[:-1], n_groups, y.shape[-1] // n_groups)
    yf = yf * lax.rsqrt(jnp.mean(yf * yf, axis=-1, keepdims=True) + EPS)
    return (yf.reshape(y.shape) * g.astype(jnp.float32)).astype(y.dtype)


def ada_params(cond, w, b):
    m = (jax.nn.silu(cond) @ w + b)[..., None, :]
    return jnp.split(m, N_MOD, axis=-1)


def modulate(h, shift, scale):
    return h * (1.0 + scale) + shift


def swiglu(h, w13, w2):
    a, g = jnp.split(h @ w13, 2, axis=-1)
    return (jax.nn.silu(a) * g) @ w2


def centred_dwconv(u, w, bias):
    out = lax.conv_general_dilated(
        u, w[:, None, :].astype(u.dtype), window_strides=(1,),
        padding=[(D_CONV // 2, D_CONV // 2)], dimension_numbers=("NWC", "WIO", "NWC"),
        feature_group_count=u.shape[-1])
    return out + bias


def segsum(a):
    cs = jnp.cumsum(a, axis=-1)
    seg = cs[..., :, None] - cs[..., None, :]
    n = a.shape[-1]
    mask = jnp.tril(jnp.ones((n, n), dtype=bool))
    return jnp.where(mask, seg, -jnp.inf)


def ssd_chunked(X, dA, Bm, Cm, init_state, with_output):
    b, l, h, p = X.shape
    g, n = Bm.shape[-2:]
    r = h // g
    nc = l // SSD_CHUNK
    Xc = X.reshape(b, nc, SSD_CHUNK, g, r, p)
    Bc = Bm.reshape(b, nc, SSD_CHUNK, g, n)
    Cc = Cm.reshape(b, nc, SSD_CHUNK, g, n)
    A = dA.reshape(b, nc, SSD_CHUNK, g, r).transpose(0, 1, 3, 4, 2)
    A_cs = jnp.cumsum(A, axis=-1)
    decay_to_end = jnp.exp(A_cs[..., -1:] - A_cs).transpose(0, 1, 4, 2, 3)
    chunk_states = jnp.einsum("bclgn,bclgrp->cbgrpn", Bc, Xc * decay_to_end[..., None])
    chunk_decay = jnp.exp(A_cs[..., -1]).transpose(1, 0, 2, 3)

    def carry_step(state, inp):
        s_chunk, d_chunk = inp
        return state * d_chunk[..., None, None] + s_chunk, state

    final, entering = lax.scan(carry_step, init_state, (chunk_states, chunk_decay))
    if not with_output:
        return None, final
    CB = jnp.einsum("bclgn,bcsgn->bcgls", Cc, Bc)
    M = CB[:, :, :, None] * jnp.exp(segsum(A))
    y_diag = jnp.einsum("bcgrls,bcsgrp->bclgrp", M, Xc)
    y_off = jnp.einsum("bclgn,cbgrpn->bclgrp", Cc, entering) * jnp.exp(A_cs).transpose(0, 1, 4, 2, 3)[..., None]
    return (y_diag + y_off).reshape(b, l, h, p), final


def ssd_branch(z, xbc, dt_raw_f, dt_raw_b, conv_w, conv_b, a_log, dt_bias, d_skip, norm_g, w_proj,
               init_f, init_b, with_output):
    b, l = z.shape[:2]
    xbc = jax.nn.silu(centred_dwconv(xbc, conv_w, conv_b))
    xs, Bm, Cm = split_cols(xbc, (D_INNER, BC_DIM, BC_DIM))
    xs = xs.reshape(b, l, SSD_HEADS, SSD_HEAD_DIM)
    Bm = Bm.reshape(b, l, SSD_GROUPS, D_STATE)
    Cm = Cm.reshape(b, l, SSD_GROUPS, D_STATE)
    ys, finals = [], []
    for d, (dt_raw, init, rev) in enumerate(((dt_raw_f, init_f, False), (dt_raw_b, init_b, True))):
        dt = jax.nn.softplus(dt_raw.astype(jnp.float32) + dt_bias[d].astype(jnp.float32))
        dA = dt * (-jnp.exp(a_log[d].astype(jnp.float32)))
        X = xs * dt[..., None]
        Bd, Cd = Bm, Cm
        if rev:
            X, dA, Bd, Cd = X[:, ::-1], dA[:, ::-1], Bd[:, ::-1], Cd[:, ::-1]
        y, fin = ssd_chunked(X, dA, Bd, Cd, init, with_output)
        finals.append(fin)
        if with_output:
            ys.append(y[:, ::-1] if rev else y)
    if not with_output:
        return None, finals[0], finals[1]
    y = (ys[0] + ys[1]).astype(z.dtype) + xs * d_skip[:, None]
    y = y.reshape(b, l, D_INNER) * jax.nn.silu(z)
    return group_rms_norm(y, norm_g, SSD_GROUPS) @ w_proj, finals[0], finals[1]


def axial_rope_tables(n_tok):
    rows = n_tok // GRID_W
    row = jnp.broadcast_to(jnp.arange(rows)[:, None], (rows, GRID_W)).reshape(-1).astype(jnp.float32)
    col = jnp.broadcast_to(jnp.arange(GRID_W)[None, :], (rows, GRID_W)).reshape(-1).astype(jnp.float32)
    inv_freq = ROPE_THETA ** (-(jnp.arange(0, ROPE_AXIS_DIM, 2, dtype=jnp.float32) / ROPE_AXIS_DIM))
    ang = jnp.stack([row[:, None] * inv_freq, col[:, None] * inv_freq], axis=1)
    return jnp.cos(ang), jnp.sin(ang)


def apply_axial_rope(x, cos, sin):
    xr = x.astype(jnp.float32).reshape(*x.shape[:-1], 2, 2, ATT_HEAD_DIM // 4)
    x1, x2 = xr[..., 0, :], xr[..., 1, :]
    c, s = cos[None, :, None], sin[None, :, None]
    out = jnp.stack([x1 * c - x2 * s, x2 * c + x1 * s], axis=-2)
    return out.reshape(x.shape).astype(x.dtype)


def attn_heads(q, k, v, qk_norm_g):
    b, l = q.shape[:2]
    q = rms_norm(q.reshape(b, l, N_Q_HEADS, ATT_HEAD_DIM), qk_norm_g[0])
    k = rms_norm(k.reshape(b, l, N_KV_HEADS, ATT_HEAD_DIM), qk_norm_g[1])
    return q, k, v.reshape(b, l, N_KV_HEADS, ATT_HEAD_DIM)


def gqa_core(qg, k, v):
    s = jnp.einsum("bqkgd,bskd->bkgqs", qg, k).astype(jnp.float32) * ATT_SCALE
    p = jax.nn.softmax(s, axis=-1).astype(v.dtype)
    return jnp.einsum("bkgqs,bskd->bqkgd", p, v)


def blocked_latent_attention(q, k_all, v_all):
    b, l = q.shape[:2]
    nb = l // Q_BLOCK
    qb = q.reshape(b, nb, Q_BLOCK, N_KV_HEADS, Q_PER_KV, ATT_HEAD_DIM).transpose(1, 0, 2, 3, 4, 5)
    ob = lax.map(lambda blk: gqa_core(blk, k_all, v_all), qb)
    return ob.transpose(1, 0, 2, 3, 4, 5).reshape(b, l, ATT_DIM)


def token_mixer(h, hc, w_in, conv_w, conv_b, a_log, dt_bias, d_skip, ssd_norm_g, w_ssd_out,
                qk_norm_g, w_attn_out, w_out, cos, sin, with_ctx_out):
    b = h.shape[0]
    z, xbc, dtf, dtb, q, k, v, g_ssd, g_att = split_cols(h @ w_in, IN_SPLITS)
    zc, xbcc, dtfc, dtbc, qc, kc, vc, g_ssd_c, g_att_c = split_cols(hc @ w_in, IN_SPLITS)
    ssd_w = (conv_w, conv_b, a_log, dt_bias, d_skip, ssd_norm_g, w_ssd_out)
    zero_state = jnp.zeros((b, SSD_GROUPS, SSD_HEADS_PER_GROUP, SSD_HEAD_DIM, D_STATE), jnp.float32)
    yc_ssd, state_f, state_b = ssd_branch(zc, xbcc, dtfc, dtbc, *ssd_w, zero_state, zero_state, with_ctx_out)
    y_ssd, _, _ = ssd_branch(z, xbc, dtf, dtb, *ssd_w, state_f, state_b, True)

    qc, kc, vc = attn_heads(qc, kc, vc, qk_norm_g)
    q, k, v = attn_heads(q, k, v, qk_norm_g)
    q = apply_axial_rope(q, cos, sin)
    k = apply_axial_rope(k, cos, sin)
    k_all = jnp.concatenate([kc, k], axis=1)
    v_all = jnp.concatenate([vc, v], axis=1)
    y_att = blocked_latent_attention(q, k_all, v_all) @ w_attn_out
    u = (jax.nn.sigmoid(g_ssd) * y_ssd + jax.nn.sigmoid(g_att) * y_att) @ w_out
    if not with_ctx_out:
        return u, None
    bc, lc = qc.shape[:2]
    yc_att = gqa_core(qc.reshape(bc, lc, N_KV_HEADS, Q_PER_KV, ATT_HEAD_DIM), kc, vc).reshape(bc, lc, ATT_DIM) @ w_attn_out
    uc = (jax.nn.sigmoid(g_ssd_c) * yc_ssd + jax.nn.sigmoid(g_att_c) * yc_att) @ w_out
    return u, uc


def setup_inputs(seed: int = 0) -> dict:
    key = jax.random.key(seed)
    ks = jax.random.split(key, 24)
    L, D = DEPTH, D_MODEL

    def nrm(k, shape, scale):
        return jax.random.normal(k, shape, jnp.float32) * scale

    dt0 = jnp.exp(jax.random.uniform(ks[13], (L, 2, SSD_HEADS), jnp.float32, math.log(1e-3), math.log(1e-1)))
    return {
        "x": nrm(ks[0], (BATCH, SEQ, D), 1.0),
        "c": nrm(ks[1], (BATCH, D), 1.0),
        "ctx": nrm(ks[2], (BATCH, CTX_LEN, D), 1.0),
        "c_ctx": nrm(ks[3], (D,), 1.0),
        "w_ada": nrm(ks[4], (L, D, N_MOD * D), 0.2 * D ** -0.5),
        "b_ada": nrm(ks[5], (L, N_MOD * D), 0.02),
        "norm_g": 1.0 + nrm(ks[6], (L, 3, D), 0.05),
        "ffn1_w13": nrm(ks[7], (L, D, 2 * D_FF), D ** -0.5),
        "ffn1_w2": nrm(ks[8], (L, D_FF, D), D_FF ** -0.5),
        "w_in": nrm(ks[9], (L, D, IN_DIM), D ** -0.5),
        "conv_w": nrm(ks[10], (L, D_CONV, CONV_DIM), D_CONV ** -0.5),
        "conv_b": nrm(ks[11], (L, CONV_DIM), 0.02),
        "a_log": jnp.log(jax.random.uniform(ks[12], (L, 2, SSD_HEADS), jnp.float32, 1.0, 16.0)),
        "dt_bias": dt0 + jnp.log(-jnp.expm1(-dt0)),
        "d_skip": 1.0 + nrm(ks[14], (L, SSD_HEADS), 0.1),
        "ssd_norm_g": 1.0 + nrm(ks[15], (L, D_INNER), 0.05),
        "w_ssd_out": nrm(ks[16], (L, D_INNER, D), D_INNER ** -0.5),
        "qk_norm_g": 1.0 + nrm(ks[17], (L, 2, ATT_HEAD_DIM), 0.05),
        "w_attn_out": nrm(ks[18], (L, ATT_DIM, D), ATT_DIM ** -0.5),
        "w_out": nrm(ks[19], (L, D, D), D ** -0.5),
        "ffn2_w13": nrm(ks[20], (L, D, 2 * D_FF), D ** -0.5),
        "ffn2_w2": nrm(ks[21], (L, D_FF, D), D_FF ** -0.5),
    }


def reference(x, c, ctx, c_ctx, w_ada, b_ada, norm_g, ffn1_w13, ffn1_w2, w_in, conv_w, conv_b,
              a_log, dt_bias, d_skip, ssd_norm_g, w_ssd_out, qk_norm_g, w_attn_out, w_out,
              ffn2_w13, ffn2_w2):
    cos, sin = axial_rope_tables(x.shape[1])
    xc = ctx
    for i in range(DEPTH):
        last = i == DEPTH - 1
        sh1, sc1, g1, sh2, sc2, g2, sh3, sc3, g3 = ada_params(c, w_ada[i], b_ada[i])
        csh1, csc1, cg1, csh2, csc2, cg2, csh3, csc3, cg3 = ada_params(c_ctx, w_ada[i], b_ada[i])
        x = x + 0.5 * g1 * swiglu(modulate(rms_norm(x, norm_g[i, 0]), sh1, sc1), ffn1_w13[i], ffn1_w2[i])
        xc = xc + 0.5 * cg1 * swiglu(modulate(rms_norm(xc, norm_g[i, 0]), csh1, csc1), ffn1_w13[i], ffn1_w2[i])
        h = modulate(rms_norm(x, norm_g[i, 1]), sh2, sc2)
        hc = modulate(rms_norm(xc, norm_g[i, 1]), csh2, csc2)
        u, uc = token_mixer(h, hc, w_in[i], conv_w[i], conv_b[i], a_log[i], dt_bias[i], d_skip[i],
                            ssd_norm_g[i], w_ssd_out[i], qk_norm_g[i], w_attn_out[i], w_out[i],
                            cos, sin, not last)
        x = x + g2 * u
        x = x + 0.5 * g3 * swiglu(modulate(rms_norm(x, norm_g[i, 2]), sh3, sc3), ffn2_w13[i], ffn2_w2[i])
        if not last:
            xc = xc + cg2 * uc
            xc = xc + 0.5 * cg3 * swiglu(modulate(rms_norm(xc, norm_g[i, 2]), csh3, csc3), ffn2_w13[i], ffn2_w2[i])
    return x
```

```python
from contextlib import ExitStack
import numpy as np
import concourse.bass as bass
import concourse.mybir as mybir
from concourse.bass_utils import run_bass_kernel_spmd

F32 = mybir.dt.float32
BF16 = mybir.dt.bfloat16
AF = mybir.ActivationFunctionType
ALU = mybir.AluOpType

ENGS = ("pe", "act", "dve", "pool", "sp")
EPS = 1e-6


class Buf:
    __slots__ = ("name", "w", "r", "dram")

    def __init__(self, name="", dram=False):
        self.name = name
        self.w = {}
        self.r = {}
        self.dram = dram


class Op:
    __slots__ = ("eng", "emit", "deps", "signal", "sigcount", "dsem", "dcount", "key")

    def __init__(self, eng, emit):
        self.eng = eng
        self.emit = emit
        self.deps = []
        self.signal = False
        self.sigcount = 0
        self.dsem = None
        self.dcount = 0
        self.key = eng


class Prog:
    def __init__(self, nc):
        self.nc = nc
        self.es = ExitStack()
        self.streams = {e: [] for e in ENGS}
        self.esem = {e: self.es.enter_context(nc.semaphore("es_" + e)) for e in ENGS}
        self.dma_counts = {}
        self.last_dma = {}
        self.last_compute = {}

    def sem(self, name):
        s = self.es.enter_context(self.nc.semaphore(name))
        self.dma_counts[id(s)] = 0
        return s

    def _rec(self, eng, emit, reads, writes, dsem=None):
        op = Op(eng, emit)
        isdma = dsem is not None
        if isdma:
            op.dsem = dsem
            self.dma_counts[id(dsem)] += 16
            op.dcount = self.dma_counts[id(dsem)]
            op.key = ("dma", id(dsem))
            self.last_dma[id(dsem)] = op
        else:
            self.last_compute[eng] = op
        key = op.key
        deps = {}
        for b in reads:
            for k, d in b.w.items():
                deps[id(d)] = d
        for b in writes:
            for k, d in b.w.items():
                if isdma:
                    if b.dram and d.dsem is not None:
                        continue
                    deps[id(d)] = d
                elif k != key:
                    deps[id(d)] = d
            for k, d in b.r.items():
                if isdma or k != key:
                    deps[id(d)] = d
        for d in deps.values():
            d.signal = True
            op.deps.append(d)
        for b in reads:
            b.r[key] = op
        for b in writes:
            b.w[key] = op
        self.streams[eng].append(op)
        return op

    def pe(self, emit, reads=(), writes=()):
        return self._rec("pe", emit, reads, writes)

    def act(self, emit, reads=(), writes=()):
        return self._rec("act", emit, reads, writes)

    def dve(self, emit, reads=(), writes=()):
        return self._rec("dve", emit, reads, writes)

    def pool(self, emit, reads=(), writes=()):
        return self._rec("pool", emit, reads, writes)

    def dma(self, queue, out, in_, sem, reads=(), writes=()):
        def emit(e, out=out, in_=in_):
            return e.dma_start(out=out, in_=in_)
        return self._rec(queue, emit, reads, writes, dsem=sem)

    def barrier(self):
        r1 = []
        for e in ENGS:
            op = Op(e, lambda eng: eng.nop())
            deps = []
            if e in self.last_compute:
                deps.append(self.last_compute[e])
            if e == "sp":
                deps.extend(self.last_dma.values())
            for d in deps:
                d.signal = True
                op.deps.append(d)
            self.streams[e].append(op)
            self.last_compute[e] = op
            r1.append(op)
        for e in ENGS:
            op = Op(e, lambda eng: eng.nop())
            for d in r1:
                if d.eng != e:
                    d.signal = True
                    op.deps.append(d)
            self.streams[e].append(op)
            self.last_compute[e] = op

    def finalize(self):
        nc = self.nc
        for e in ENGS:
            c = 0
            for op in self.streams[e]:
                if op.dsem is None and op.signal:
                    c += 1
                    op.sigcount = c
        self.stats = {}
        with nc.Block() as block:
            def run(ename, eng):
                seen = {}
                nwait = 0
                for op in self.streams[ename]:
                    waits = {}
                    for d in op.deps:
                        if d.dsem is not None:
                            s, v = d.dsem, d.dcount
                        else:
                            s, v = self.esem[d.eng], d.sigcount
                        k = id(s)
                        if seen.get(k, 0) >= v:
                            continue
                        if k not in waits or waits[k][1] < v:
                            waits[k] = (s, v)
                    for k, (s, v) in waits.items():
                        eng.wait_ge(s, v)
                        seen[k] = v
                        nwait += 1
                    ins = op.emit(eng)
                    if op.dsem is not None:
                        ins.then_inc(op.dsem, 16)
                    elif op.signal:
                        ins.then_inc(self.esem[ename], 1)
                self.stats[ename] = (len(self.streams[ename]), nwait)

            @block.tensor
            def _(eng):
                run("pe", eng)

            @block.scalar
            def _(eng):
                run("act", eng)

            @block.vector
            def _(eng):
                run("dve", eng)

            @block.gpsimd
            def _(eng):
                run("pool", eng)

            @block.sync
            def _(eng):
                run("sp", eng)
        self.es.close()


D = 1024
KC = 8
DFF = 2816
HC = 22
NMOD = 9
CTX = 256
DIN = 2048
CONV = 3072
INDIM = 8768
C_Z, C_XBC, C_DT, C_Q, C_K, C_V, C_GS, C_GA = 0, 2048, 5120, 5184, 6208, 6464, 6720, 7744
ATT_SCALE = 128 ** -0.5


_UC = [0]


def U(name):
    _UC[0] += 1
    return "%s_%d" % (name, _UC[0])


def C(name, *args, **kw):
    def emit(e):
        return getattr(e, name)(*args, **kw)
    return emit


class Ring:
    def __init__(self, p, stack, name, n, shape, dt):
        self.t = [stack.enter_context(p.nc.sbuf_tensor(U("%s%d" % (name, i)), list(shape), dt)) for i in range(n)]
        self.b = [Buf("%s%d" % (name, i)) for i in range(n)]
        self.s = [p.sem_cached("%s%d" % (name, i)) for i in range(n)]
        self.n = n
        self.i = 0

    def next(self):
        k = self.i % self.n
        self.i += 1
        return self.t[k], self.b[k], self.s[k]


def build_program(S, L, last_flags=None):
    if last_flags is None:
        last_flags = [i == L - 1 for i in range(L)]
    TT = CTX + S
    NCHK = TT // 128
    tiles = [(0, CTX, 1)] + [(CTX + 512 * i, 512, 0) for i in range(S // 512)]
    NT = len(tiles)

    nc = bass.Bass("TRN2", target_bir_lowering=False)
    p = Prog(nc)
    semcache = {}

    def sem_cached(name):
        if name not in semcache:
            semcache[name] = p.sem(name)
        return semcache[name]
    p.sem_cached = sem_cached

    def din(name, shape, dt=F32):
        return nc.dram_tensor(name, list(shape), dt, kind="ExternalInput").ap()

    def dscr(name, shape, dt):
        return nc.dram_tensor(name, list(shape), dt).ap()

    xT0 = din("xT0", [KC, 128, TT])
    csT = din("csT", [128, KC, 2])
    bada = din("bada", [128, L, 72])
    ng = din("ng", [128, L, 3, KC])
    w_ada = din("w_ada", [L, D, NMOD * D])
    w13 = [din("ffn1_w13", [L, D, 2 * DFF]), din("ffn2_w13", [L, D, 2 * DFF])]
    w2 = [din("ffn1_w2", [L, DFF, D]), din("ffn2_w2", [L, DFF, D])]
    w_in = din("w_in", [L, D, INDIM])
    convw = din("convw", [128, L, 3, 24])
    convb = din("convb", [128, L, 24])
    alog = din("alog", [128, L, 64])
    dtb = din("dtb", [128, L, 64])
    dskip = din("dskip", [128, L, 32])
    sng = din("sng", [128, L, DIN])
    w_so = din("w_ssd_out", [L, DIN, D])
    qkg = din("qkg", [128, L, 4])
    w_ao = din("w_attn_out", [L, D, D])
    w_o = din("w_out", [L, D, D])
    cosT = din("cosT", [128, TT])
    sinT = din("sinT", [128, TT])
    identd = din("ident", [128, 128])
    masksd = din("masks", [128, 4, 128])
    outT = nc.dram_tensor("outT", [KC, 128, S], F32, kind="ExternalOutput").ap()

    xT = dscr("xT", [KC, 128, TT], F32)
    uT = dscr("uT", [HC, 128, TT], BF16)
    z_tok = dscr("z_tok", [TT, DIN], F32)
    gT = dscr("gT", [16, 128, TT], F32)
    xbcp = dscr("xbcp", [24, 128, TT], F32)
    dtraw = dscr("dtraw", [TT, 64], F32)
    xsT = dscr("xsT", [16, 128, TT], BF16)
    bcT = dscr("bcT", [8, 128, TT], BF16)
    xb_tok = dscr("xb_tok", [TT, 2560], BF16)
    qT = dscr("qT", [8, 128, TT], BF16)
    kT = dscr("kT", [2, 128, TT], BF16)
    v_tok = dscr("v_tok", [TT, 256], BF16)
    sin_d = [dscr("sinf", [NCHK, 4, 128, 512], BF16), dscr("sinb", [NCHK, 4, 128, 512], BF16)]
    yT = dscr("yT", [16, 128, TT], BF16)
    attT = dscr("attT", [8, 128, TT], BF16)

    B_xT = [Buf("xT%d" % i, True) for i in range(NT)]
    B_x0 = Buf("xT0", True)
    B_uT = [Buf("uT%d" % i, True) for i in range(NT)]
    B_z = Buf("z", True); B_g = Buf("g", True); B_xbcp = Buf("xbcp", True); B_dtraw = Buf("dtraw", True)
    B_xsT = Buf("xsT", True); B_bcT = Buf("bcT", True); B_xbtok = Buf("xbtok", True)
    B_q = Buf("q", True); B_k = Buf("k", True); B_v = Buf("v", True)
    B_sin = [Buf("sinf", True), Buf("sinb", True)]
    B_yT = Buf("yT", True); B_att = Buf("att", True); B_out = Buf("out", True)
    B_const = Buf("const", True)

    gs = p.es
    MOD = gs.enter_context(nc.sbuf_tensor(U("MOD"), [128, L, 72, 2], F32)); B_MOD = Buf("MOD")
    AS = gs.enter_context(nc.sbuf_tensor(U("AS"), [128, L, 3, KC, 2], F32))
    HG = gs.enter_context(nc.sbuf_tensor(U("HG"), [128, L, 3, KC, 2], F32))
    NG = gs.enter_context(nc.sbuf_tensor(U("NG"), [128, L, 3, KC], F32))
    ones_b = gs.enter_context(nc.sbuf_tensor(U("ones_b"), [128, 128], BF16))
    ones_f = gs.enter_context(nc.sbuf_tensor(U("ones_f"), [128, 128], F32))
    ident_b = gs.enter_context(nc.sbuf_tensor(U("ident_b"), [128, 128], BF16))
    masks = gs.enter_context(nc.sbuf_tensor(U("masks_s"), [128, 4, 128], F32))
    B_glob = Buf("glob")
    PS = [gs.enter_context(nc.psum_tensor("ps%d" % i, [128, 512], F32)) for i in range(7)]
    B_PS = [Buf("ps%d" % i) for i in range(7)]
    PSB = gs.enter_context(nc.psum_tensor("psb", [128, 1024], BF16)); B_PSB = Buf("psb")
    s_misc = sem_cached("misc")

    def MODap(l, j, ch, which):
        return MOD[:, l, j * 8 + ch, which:which + 1]

    def phase_ada():
        with ExitStack() as st:
            cs = st.enter_context(nc.sbuf_tensor(U("cs"), [128, KC, 2], F32)); B_cs = Buf()
            cs2 = st.enter_context(nc.sbuf_tensor(U("cs2"), [128, KC, 2], F32)); B_cs2 = Buf()
            bd = st.enter_context(nc.sbuf_tensor(U("bd"), [128, L, 72], F32)); B_bd = Buf()
            p.dma("sp", cs[:], csT, s_misc, reads=[B_const], writes=[B_cs])
            p.dma("sp", bd[:], bada, s_misc, reads=[B_const], writes=[B_bd])
            p.dma("sp", NG[:], ng, s_misc, reads=[B_const], writes=[B_glob])
            p.dma("sp", masks[:], masksd, s_misc, reads=[B_const], writes=[B_glob])
            p.dma("pool", ident_b[:], identd, s_misc, reads=[B_const], writes=[B_glob])
            p.dve(C("memset", ones_b[:], 1.0), writes=[B_glob])
            p.dve(C("memset", ones_f[:], 1.0), writes=[B_glob])
            p.act(C("activation", out=cs2[:], in_=cs[:], func=AF.Silu), reads=[B_cs], writes=[B_cs2])
            ring = Ring(p, st, "wada", 2, [128, KC, 1024], F32)
            for l in range(L):
                wv = w_ada[l].rearrange("(k p) n -> p k n", p=128)
                for j in range(NMOD):
                    wt, wb, ws = ring.next()
                    p.dma("sp", wt[:], wv[:, :, j * 1024:(j + 1) * 1024], ws, reads=[B_const], writes=[wb])
                    for ch in range(8):
                        col = (j * 8 + ch) * 2
                        for kc in range(KC):
                            p.pe(C("matmul",
                                PS[0][:, col:col + 2], lhsT=wt[:, kc, ch * 128:(ch + 1) * 128], rhs=cs2[:, kc, :],
                                start=(kc == 0), stop=(kc == KC - 1)), reads=[wb, B_cs2], writes=[B_PS[0]])
                p.dve(C("tensor_tensor",
                    out=MOD[:, l, :, :], in0=PS[0][:, 0:144].rearrange("p (a b) -> p a b", b=2),
                    in1=bd[:, l, :].unsqueeze(2).to_broadcast([128, 72, 2]), op=ALU.add),
                    reads=[B_PS[0], B_bd], writes=[B_MOD])
                for s in range(3):
                    sc = MOD[:, l, (3 * s + 1) * 8:(3 * s + 2) * 8, :]
                    gt = MOD[:, l, (3 * s + 2) * 8:(3 * s + 3) * 8, :]
                    p.dve(C("scalar_tensor_tensor",
                        out=AS[:, l, s, :, :], in0=sc, scalar=1.0,
                        in1=NG[:, l, s, :].unsqueeze(2).to_broadcast([128, KC, 2]), op0=ALU.add, op1=ALU.mult),
                        reads=[B_MOD, B_glob], writes=[B_MOD])
                    p.dve(C("tensor_scalar",
                        out=HG[:, l, s, :, :], in0=gt, scalar1=(1.0 if s == 1 else 0.5), scalar2=None, op0=ALU.mult),
                        reads=[B_MOD], writes=[B_MOD])
            for ti, (t0, T, wh) in enumerate(tiles):
                p.dma("sp", xT[:, :, t0:t0 + T], xT0[:, :, t0:t0 + T], s_misc, reads=[B_x0], writes=[B_xT[ti]])
            p.barrier()

    def norm_mod(xt, B_x, h, B_h, sqr, rs, B_rs, tmpr, T, l, s, wh, psb):
        for c in range(KC):
            sq, B_sq, _ = sqr.next()
            p.act(C("activation", out=sq[:, :T], in_=xt[:, c, :T], func=AF.Square),
                  reads=[B_x], writes=[B_sq])
            p.pe(C("matmul", PS[psb][:, :T], lhsT=ones_b[:], rhs=sq[:, :T], start=(c == 0), stop=(c == KC - 1)),
                 reads=[B_sq, B_glob], writes=[B_PS[psb]])
        p.act(C("activation", out=rs[:, :T], in_=PS[psb][:, :T], func=AF.Sqrt, bias=EPS, scale=1.0 / D),
              reads=[B_PS[psb]], writes=[B_rs])
        p.dve(C("reciprocal", out=rs[:, :T], in_=rs[:, :T]), reads=[B_rs], writes=[B_rs])
        for c in range(KC):
            tmp, B_tmp, _ = tmpr.next()
            p.dve(C("scalar_tensor_tensor",
                out=tmp[:, :T], in0=xt[:, c, :T], scalar=AS[:, l, s, c, wh:wh + 1], in1=rs[:, :T],
                op0=ALU.mult, op1=ALU.mult), reads=[B_x, B_rs, B_MOD], writes=[B_tmp])
            p.act(C("activation", out=h[:, c, :T], in_=tmp[:, :T], func=AF.Identity,
                                              bias=MODap(l, 3 * s, c, wh), scale=1.0),
                  reads=[B_tmp, B_MOD], writes=[B_h])

    def load_w(dst, B_dst, src_ap, sem):
        p.dma("pool", dst, src_ap, sem, reads=[B_const], writes=[B_dst])

    def phase_ffn(l, f, s, tsel, final):
        with ExitStack() as st:
            W = st.enter_context(nc.sbuf_tensor(U("W13"), [128, KC, 2 * DFF], BF16)); B_W = Buf()
            sW = sem_cached("wA")
            wv = w13[f][l].rearrange("(k p) n -> p k n", p=128)
            for kc in range(KC):
                load_w(W[:, kc, :], B_W, wv[:, kc, :], sW)
            xr = Ring(p, st, "xa", 2, [128, KC, 512], F32)
            hr = Ring(p, st, "hA", 2, [128, KC, 512], BF16)
            sqr = Ring(p, st, "sqr", 2, [128, 512], BF16)
            tmpr = Ring(p, st, "tmpr", 2, [128, 512], F32)
            rs = st.enter_context(nc.sbuf_tensor(U("rs"), [128, 512], F32)); B_rs = Buf()
            sa = [st.enter_context(nc.sbuf_tensor(U("sa%d" % i), [128, 512], F32)) for i in range(2)]
            B_sa = [Buf(), Buf()]
            ur = Ring(p, st, "ua", 2, [128, HC, 512], BF16)
            for ti in tsel:
                t0, T, wh = tiles[ti]
                xt, B_x, sx = xr.next()
                p.dma("sp", xt[:, :, :T], xT[:, :, t0:t0 + T].rearrange("c p t -> p c t"), sx, reads=[B_xT[ti]], writes=[B_x])
                h, B_h, _ = hr.next()
                norm_mod(xt, B_x, h, B_h, sqr, rs, B_rs, tmpr, T, l, s, wh, 6)
                ut, B_u, su = ur.next()
                for hc in range(HC):
                    pa, pg = (0, 1) if hc % 2 == 0 else (2, 3)
                    for kc in range(KC):
                        p.pe(C("matmul",
                            PS[pa][:, :T], lhsT=W[:, kc, hc * 128:(hc + 1) * 128], rhs=h[:, kc, :T],
                            start=(kc == 0), stop=(kc == KC - 1)), reads=[B_W, B_h], writes=[B_PS[pa]])
                    for kc in range(KC):
                        p.pe(C("matmul",
                            PS[pg][:, :T], lhsT=W[:, kc, DFF + hc * 128:DFF + (hc + 1) * 128], rhs=h[:, kc, :T],
                            start=(kc == 0), stop=(kc == KC - 1)), reads=[B_W, B_h], writes=[B_PS[pg]])
                    k2 = hc % 2
                    p.act(C("activation", out=sa[k2][:, :T], in_=PS[pa][:, :T], func=AF.Silu),
                          reads=[B_PS[pa]], writes=[B_sa[k2]])
                    p.dve(C("tensor_tensor",
                        out=ut[:, hc, :T], in0=sa[k2][:, :T], in1=PS[pg][:, :T], op=ALU.mult),
                        reads=[B_sa[k2], B_PS[pg]], writes=[B_u])
                p.dma("pool", uT[:, :, t0:t0 + T].rearrange("c p t -> p c t"), ut[:, :, :T], su, reads=[B_u], writes=[B_uT[ti]])
            p.barrier()
        with ExitStack() as st:
            W = st.enter_context(nc.sbuf_tensor(U("W2"), [128, HC, D], BF16)); B_W = Buf()
            sW = sem_cached("wA")
            load_w(W[:], B_W, w2[f][l].rearrange("(k p) n -> p k n", p=128), sW)
            xr = Ring(p, st, "xb", 2, [128, KC, 512], F32)
            ur = Ring(p, st, "ub", 2, [128, HC, 512], BF16)
            for ti in tsel:
                t0, T, wh = tiles[ti]
                xt, B_x, sx = xr.next()
                ut, B_u, su = ur.next()
                p.dma("sp", ut[:, :, :T], uT[:, :, t0:t0 + T].rearrange("c p t -> p c t"), su, reads=[B_uT[ti]], writes=[B_u])
                p.dma("sp", xt[:, :, :T], xT[:, :, t0:t0 + T].rearrange("c p t -> p c t"), sx, reads=[B_xT[ti]], writes=[B_x])
                for n in range(KC):
                    pb = n % 4
                    for kc in range(HC):
                        p.pe(C("matmul",
                            PS[pb][:, :T], lhsT=W[:, kc, n * 128:(n + 1) * 128], rhs=ut[:, kc, :T],
                            start=(kc == 0), stop=(kc == HC - 1)), reads=[B_W, B_u], writes=[B_PS[pb]])
                    p.dve(C("scalar_tensor_tensor",
                        out=xt[:, n, :T], in0=PS[pb][:, :T], scalar=HG[:, l, s, n, wh:wh + 1], in1=xt[:, n, :T],
                        op0=ALU.mult, op1=ALU.add), reads=[B_PS[pb], B_x, B_MOD], writes=[B_x])
                if final and wh == 0:
                    p.dma("pool", outT[:, :, t0 - CTX:t0 - CTX + T].rearrange("c p t -> p c t"), xt[:, :, :T], sx,
                          reads=[B_x], writes=[B_out])
                else:
                    p.dma("pool", xT[:, :, t0:t0 + T].rearrange("c p t -> p c t"), xt[:, :, :T], sx,
                          reads=[B_x], writes=[B_xT[ti]])
            p.barrier()

    def inproj_common(st):
        xr = Ring(p, st, "xc", 2, [128, KC, 512], F32)
        hr = Ring(p, st, "hB", 2, [128, KC, 512], BF16)
        sqr = Ring(p, st, "sqr", 2, [128, 512], BF16)
        tmpr = Ring(p, st, "tmpr", 2, [128, 512], F32)
        rs = st.enter_context(nc.sbuf_tensor(U("rs"), [128, 512], F32)); B_rs = Buf()

        def prep(ti, l):
            t0, T, wh = tiles[ti]
            xt, B_x, sx = xr.next()
            p.dma("sp", xt[:, :, :T], xT[:, :, t0:t0 + T].rearrange("c p t -> p c t"), sx, reads=[B_xT[ti]], writes=[B_x])
            h, B_h, _ = hr.next()
            norm_mod(xt, B_x, h, B_h, sqr, rs, B_rs, tmpr, T, l, 1, wh, 6)
            return h, B_h
        return prep

    def phase_inproj1(l):
        with ExitStack() as st:
            W = st.enter_context(nc.sbuf_tensor(U("Wz"), [128, KC, 4096], BF16)); B_W = Buf()
            sW = sem_cached("wA")
            wv = w_in[l].rearrange("(k p) n -> p k n", p=128)
            load_w(W[:, :, 0:2048], B_W, wv[:, :, C_Z:C_Z + 2048], sW)
            load_w(W[:, :, 2048:4096], B_W, wv[:, :, C_GS:C_GS + 2048], sW)
            prep = inproj_common(st)
            zr = Ring(p, st, "zo", 2, [128, 2048], F32)
            gr = Ring(p, st, "go", 2, [128, 4, 512], F32)
            for ti in range(NT):
                t0, T, wh = tiles[ti]
                h, B_h = prep(ti, l)
                for sub in range(T // 128):
                    zt, B_zt, sz = zr.next()
                    for cb in range(4):
                        pb = cb % 4
                        for kc in range(KC):
                            p.pe(C("matmul",
                                PS[pb][:, :], lhsT=h[:, kc, sub * 128:(sub + 1) * 128], rhs=W[:, kc, cb * 512:(cb + 1) * 512],
                                start=(kc == 0), stop=(kc == KC - 1)), reads=[B_W, B_h], writes=[B_PS[pb]])
                        p.act(C("activation", out=zt[:, cb * 512:(cb + 1) * 512], in_=PS[pb][:, :], func=AF.Silu),
                              reads=[B_PS[pb]], writes=[B_zt])
                    p.dma("pool", z_tok[t0 + sub * 128:t0 + (sub + 1) * 128, :], zt[:], sz, reads=[B_zt], writes=[B_z])
                for gq in range(4):
                    gt_, B_gt, sg = gr.next()
                    for c4 in range(4):
                        ch = gq * 4 + c4
                        pb = 4 + (c4 % 2)
                        for kc in range(KC):
                            p.pe(C("matmul",
                                PS[pb][:, :T], lhsT=W[:, kc, 2048 + ch * 128:2048 + (ch + 1) * 128], rhs=h[:, kc, :T],
                                start=(kc == 0), stop=(kc == KC - 1)), reads=[B_W, B_h], writes=[B_PS[pb]])
                        p.act(C("activation", out=gt_[:, c4, :T], in_=PS[pb][:, :T], func=AF.Sigmoid),
                              reads=[B_PS[pb]], writes=[B_gt])
                    p.dma("pool", gT[gq * 4:(gq + 1) * 4, :, t0:t0 + T].rearrange("c p t -> p c t"), gt_[:, :, :T], sg,
                          reads=[B_gt], writes=[B_g])
            p.barrier()

    def phase_inproj2(l):
        with ExitStack() as st:
            W = st.enter_context(nc.sbuf_tensor(U("Wx"), [128, KC, 3136], BF16)); B_W = Buf()
            sW = sem_cached("wA")
            wv = w_in[l].rearrange("(k p) n -> p k n", p=128)
            load_w(W[:, :, :], B_W, wv[:, :, C_XBC:C_XBC + 3136], sW)
            prep = inproj_common(st)
            orr = Ring(p, st, "xo", 2, [128, 4, 512], F32)
            dr = Ring(p, st, "do", 2, [128, 4, 64], F32)
            for ti in range(NT):
                t0, T, wh = tiles[ti]
                h, B_h = prep(ti, l)
                for gq in range(6):
                    ot, B_ot, so = orr.next()
                    for c4 in range(4):
                        ch = gq * 4 + c4
                        pb = c4 % 4
                        for kc in range(KC):
                            p.pe(C("matmul",
                                PS[pb][:, :T], lhsT=W[:, kc, ch * 128:(ch + 1) * 128], rhs=h[:, kc, :T],
                                start=(kc == 0), stop=(kc == KC - 1)), reads=[B_W, B_h], writes=[B_PS[pb]])
                        if c4 % 2 == 0:
                            p.act(C("activation", out=ot[:, c4, :T], in_=PS[pb][:, :T], func=AF.Copy),
                                  reads=[B_PS[pb]], writes=[B_ot])
                        else:
                            p.dve(C("tensor_copy", out=ot[:, c4, :T], in_=PS[pb][:, :T]),
                                  reads=[B_PS[pb]], writes=[B_ot])
                    p.dma("pool", xbcp[gq * 4:(gq + 1) * 4, :, t0:t0 + T].rearrange("c p t -> p c t"), ot[:, :, :T], so,
                          reads=[B_ot], writes=[B_xbcp])
                dt_, B_dt, sd = dr.next()
                for sub in range(T // 128):
                    for kc in range(KC):
                        p.pe(C("matmul",
                            PS[4][:, sub * 64:(sub + 1) * 64], lhsT=h[:, kc, sub * 128:(sub + 1) * 128], rhs=W[:, kc, 3072:3136],
                            start=(kc == 0), stop=(kc == KC - 1)), reads=[B_W, B_h], writes=[B_PS[4]])
                nsub = T // 128
                p.dve(C("tensor_copy",
                    out=dt_[:, :nsub, :], in_=PS[4][:, :nsub * 64].rearrange("p (a b) -> p a b", b=64)),
                    reads=[B_PS[4]], writes=[B_dt])
                p.dma("pool", dtraw[t0:t0 + T, :].rearrange("(a p) j -> p a j", p=128), dt_[:, :nsub, :], sd,
                      reads=[B_dt], writes=[B_dtraw])
            p.barrier()

    def phase_inproj3(l):
        with ExitStack() as st:
            W = st.enter_context(nc.sbuf_tensor(U("Wq"), [128, KC, 2816], BF16)); B_W = Buf()
            sW = sem_cached("wA")
            wv = w_in[l].rearrange("(k p) n -> p k n", p=128)
            load_w(W[:, :, 0:1280], B_W, wv[:, :, C_Q:C_Q + 1280], sW)
            load_w(W[:, :, 2560:2816], B_W, wv[:, :, C_V:C_V + 256], sW)
            for kc in range(KC):
                src = wv[:, kc, C_Q:C_Q + 1280].rearrange("p (a two f) -> p a two f", two=2, f=32)
                dst = W[:, kc, 1280:2560].rearrange("p (a two f) -> p a two f", two=2, f=32)
                load_w(dst[:, :, 0, :], B_W, src[:, :, 1, :], sW)
                load_w(dst[:, :, 1, :], B_W, src[:, :, 0, :], sW)
            prep = inproj_common(st)
            G = st.enter_context(nc.sbuf_tensor(U("qkg_s"), [128, 4], F32)); B_G = Buf()
            p.dma("sp", G[:], qkg[:, l, :], s_misc, reads=[B_const], writes=[B_G])
            cr = Ring(p, st, "cs_", 2, [128, 2, 512], F32)
            qo = Ring(p, st, "qo", 2, [128, 10, 512], BF16)
            vo = Ring(p, st, "vo", 2, [128, 4, 256], BF16)
            sq2 = st.enter_context(nc.sbuf_tensor(U("sq2"), [128, 512], BF16)); B_sq2 = Buf()
            rs2 = st.enter_context(nc.sbuf_tensor(U("rs2"), [128, 512], F32)); B_rs2 = Buf()
            t1 = st.enter_context(nc.sbuf_tensor(U("t1"), [128, 512], F32)); B_t1 = Buf()
            t2 = st.enter_context(nc.sbuf_tensor(U("t2"), [128, 512], F32)); B_t2 = Buf()
            for ti in range(NT):
                t0, T, wh = tiles[ti]
                h, B_h = prep(ti, l)
                ct, B_ct, sc_ = cr.next()
                p.dma("sp", ct[:, 0, :T], cosT[:, t0:t0 + T], sc_, reads=[B_const], writes=[B_ct])
                p.dma("sp", ct[:, 1, :T], sinT[:, t0:t0 + T], sc_, reads=[B_const], writes=[B_ct])
                qt, B_qt, sq_ = qo.next()
                for hd in range(10):
                    gi = 0 if hd < 8 else 1
                    pa, pr = (0, 1) if hd % 2 == 0 else (2, 3)
                    for kc in range(KC):
                        p.pe(C("matmul",
                            PS[pa][:, :T], lhsT=W[:, kc, hd * 128:(hd + 1) * 128], rhs=h[:, kc, :T],
                            start=(kc == 0), stop=(kc == KC - 1)), reads=[B_W, B_h], writes=[B_PS[pa]])
                    for kc in range(KC):
                        p.pe(C("matmul",
                            PS[pr][:, :T], lhsT=W[:, kc, 1280 + hd * 128:1280 + (hd + 1) * 128], rhs=h[:, kc, :T],
                            start=(kc == 0), stop=(kc == KC - 1)), reads=[B_W, B_h], writes=[B_PS[pr]])
                    p.act(C("activation", out=sq2[:, :T], in_=PS[pa][:, :T], func=AF.Square),
                          reads=[B_PS[pa]], writes=[B_sq2])
                    p.pe(C("matmul", PS[5][:, :T], lhsT=ones_b[:], rhs=sq2[:, :T], start=True, stop=True),
                         reads=[B_sq2, B_glob], writes=[B_PS[5]])
                    p.act(C("activation", out=rs2[:, :T], in_=PS[5][:, :T], func=AF.Sqrt, bias=EPS, scale=1.0 / 128),
                          reads=[B_PS[5]], writes=[B_rs2])
                    p.dve(C("reciprocal", out=rs2[:, :T], in_=rs2[:, :T]), reads=[B_rs2], writes=[B_rs2])
                    p.dve(C("scalar_tensor_tensor",
                        out=t1[:, :T], in0=PS[pa][:, :T], scalar=G[:, gi:gi + 1], in1=ct[:, 0, :T], op0=ALU.mult, op1=ALU.mult),
                        reads=[B_PS[pa], B_G, B_ct], writes=[B_t1])
                    p.dve(C("scalar_tensor_tensor",
                        out=t2[:, :T], in0=PS[pr][:, :T], scalar=G[:, 2 + gi:3 + gi], in1=ct[:, 1, :T], op0=ALU.mult, op1=ALU.mult),
                        reads=[B_PS[pr], B_G, B_ct], writes=[B_t2])
                    p.dve(C("tensor_tensor", out=t1[:, :T], in0=t1[:, :T], in1=t2[:, :T], op=ALU.add),
                          reads=[B_t1, B_t2], writes=[B_t1])
                    p.dve(C("tensor_tensor", out=qt[:, hd, :T], in0=t1[:, :T], in1=rs2[:, :T], op=ALU.mult),
                          reads=[B_t1, B_rs2], writes=[B_qt])
                p.dma("pool", qT[:, :, t0:t0 + T].rearrange("c p t -> p c t"), qt[:, 0:8, :T], sq_, reads=[B_qt], writes=[B_q])
                p.dma("pool", kT[:, :, t0:t0 + T].rearrange("c p t -> p c t"), qt[:, 8:10, :T], sq_, reads=[B_qt], writes=[B_k])
                vt, B_vt, sv = vo.next()
                nsub = T // 128
                for sub in range(nsub):
                    for kc in range(KC):
                        p.pe(C("matmul",
                            PS[4][:, (sub % 2) * 256:(sub % 2 + 1) * 256], lhsT=h[:, kc, sub * 128:(sub + 1) * 128], rhs=W[:, kc, 2560:2816],
                            start=(kc == 0), stop=(kc == KC - 1)), reads=[B_W, B_h], writes=[B_PS[4]])
                    p.act(C("activation", out=vt[:, sub, :], in_=PS[4][:, (sub % 2) * 256:(sub % 2 + 1) * 256], func=AF.Copy),
                          reads=[B_PS[4]], writes=[B_vt])
                p.dma("pool", v_tok[t0:t0 + T, :].rearrange("(a p) j -> p a j", p=128), vt[:, :nsub, :], sv,
                      reads=[B_vt], writes=[B_v])
            p.barrier()

    def phase_conv(l):
        with ExitStack() as st:
            cw = st.enter_context(nc.sbuf_tensor(U("cw"), [128, 3, 24], F32)); B_cw = Buf()
            cb = st.enter_context(nc.sbuf_tensor(U("cb"), [128, 24], F32))
            p.dma("sp", cw[:], convw[:, l, :, :], s_misc, reads=[B_const], writes=[B_cw])
            p.dma("sp", cb[:], convb[:, l, :], s_misc, reads=[B_const], writes=[B_cw])
            ur = Ring(p, st, "cu", 3, [128, 514], F32)
            acc = [st.enter_context(nc.sbuf_tensor(U("acc%d" % i), [128, 512], F32)) for i in range(2)]
            B_acc = [Buf(), Buf()]
            orr = Ring(p, st, "co", 2, [128, 4, 512], BF16)
            tk = Ring(p, st, "tk", 2, [128, 4, 2560], BF16)
            for ti in range(NT):
                t0, T, wh = tiles[ti]
                left = (ti <= 1)
                right = (ti == 0 or ti == NT - 1)
                tkt, B_tk, stk = tk.next()
                nsub = T // 128
                for gq in range(6):
                    ot, B_ot, so = orr.next()
                    for c4 in range(4):
                        ch = gq * 4 + c4
                        ut, B_u, su = ur.next()
                        lo = 0 if not left else 1
                        hi = T + 2 if not right else T + 1
                        if left:
                            p.dve(C("memset", ut[:, 0:1], 0.0), writes=[B_u])
                        if right:
                            p.dve(C("memset", ut[:, T + 1:T + 2], 0.0), writes=[B_u])
                        p.dma("sp", ut[:, lo:hi], xbcp[ch, :, t0 - 1 + lo:t0 - 1 + hi], su, reads=[B_xbcp], writes=[B_u])
                        a = acc[c4 % 2]; B_a = B_acc[c4 % 2]
                        p.act(C("activation", out=a[:, :T], in_=ut[:, 1:T + 1], func=AF.Identity,
                                                                       bias=cb[:, ch:ch + 1], scale=cw[:, 1, ch:ch + 1]),
                              reads=[B_u, B_cw], writes=[B_a])
                        p.dve(C("scalar_tensor_tensor",
                            out=a[:, :T], in0=ut[:, 0:T], scalar=cw[:, 0, ch:ch + 1], in1=a[:, :T], op0=ALU.mult, op1=ALU.add),
                            reads=[B_u, B_cw, B_a], writes=[B_a])
                        p.dve(C("scalar_tensor_tensor",
                            out=a[:, :T], in0=ut[:, 2:T + 2], scalar=cw[:, 2, ch:ch + 1], in1=a[:, :T], op0=ALU.mult, op1=ALU.add),
                            reads=[B_u, B_cw, B_a], writes=[B_a])
                        p.act(C("activation", out=ot[:, c4, :T], in_=a[:, :T], func=AF.Silu),
                              reads=[B_a], writes=[B_ot])
                        if ch < 20:
                            for sub in range(nsub):
                                p.pe(C("transpose",
                                    out=PSB[:, sub * 128:(sub + 1) * 128], in_=ot[:, c4, sub * 128:(sub + 1) * 128], identity=ident_b[:]),
                                    reads=[B_ot, B_glob], writes=[B_PSB])
                            p.dve(C("tensor_copy",
                                out=tkt[:, :nsub, ch * 128:(ch + 1) * 128],
                                in_=PSB[:, :nsub * 128].rearrange("p (a b) -> p a b", b=128)),
                                reads=[B_PSB], writes=[B_tk])
                    if gq < 4:
                        p.dma("pool", xsT[gq * 4:(gq + 1) * 4, :, t0:t0 + T].rearrange("c p t -> p c t"), ot[:, :, :T], so,
                              reads=[B_ot], writes=[B_xsT])
                    else:
                        p.dma("pool", bcT[(gq - 4) * 4:(gq - 3) * 4, :, t0:t0 + T].rearrange("c p t -> p c t"), ot[:, :, :T], so,
                              reads=[B_ot], writes=[B_bcT])
                p.dma("pool", xb_tok[t0:t0 + T, :].rearrange("(a p) j -> p a j", p=128), tkt[:, :nsub, :], stk,
                      reads=[B_tk], writes=[B_xbtok])
            p.barrier()

    def phase_ssd(l, last):
        NQ = NCHK * 64
        with ExitStack() as st0:
            dt = st0.enter_context(nc.sbuf_tensor(U("dt"), [128, NCHK, 64], F32)); B_dt = Buf()
            dA = st0.enter_context(nc.sbuf_tensor(U("dA"), [128, NCHK, 64], F32)); B_dA = Buf()
            cs = st0.enter_context(nc.sbuf_tensor(U("cs"), [128, NCHK, 64], F32)); B_cs = Buf()
            tot = st0.enter_context(nc.sbuf_tensor(U("tot"), [128, NCHK, 64], F32)); B_tot = Buf()
            wst = st0.enter_context(nc.sbuf_tensor(U("wst"), [128, NCHK, 64], F32)); B_wst = Buf()
            eoff = st0.enter_context(nc.sbuf_tensor(U("eoff"), [128, NCHK, 64], F32)); B_eoff = Buf()
            cdec = st0.enter_context(nc.sbuf_tensor(U("cdec"), [128, NCHK, 64], F32)); B_cdec = Buf()
            with ExitStack() as st:
                al = st.enter_context(nc.sbuf_tensor(U("al"), [128, 64], F32)); B_al = Buf()
                db = st.enter_context(nc.sbuf_tensor(U("db"), [128, 64], F32)); B_db = Buf()
                tmpa = st.enter_context(nc.sbuf_tensor(U("tmpa"), [128, NCHK, 64], F32)); B_tmpa = Buf()
                p.dma("sp", al[:], alog[:, l, :], s_misc, reads=[B_const], writes=[B_al])
                p.dma("sp", db[:], dtb[:, l, :], s_misc, reads=[B_const], writes=[B_db])
                p.dma("sp", dt[:], dtraw.rearrange("(c p) j -> p c j", p=128), s_misc, reads=[B_dtraw], writes=[B_dt])
                p.act(C("activation", out=al[:], in_=al[:], func=AF.Exp), reads=[B_al], writes=[B_al])
                p.dve(C("tensor_tensor", out=dt[:], in0=dt[:], in1=db[:].unsqueeze(1).to_broadcast([128, NCHK, 64]), op=ALU.add),
                      reads=[B_dt, B_db], writes=[B_dt])
                p.act(C("activation", out=dt[:], in_=dt[:], func=AF.Exp), reads=[B_dt], writes=[B_dt])
                p.act(C("activation", out=dt[:], in_=dt[:], func=AF.Ln, bias=1.0, scale=1.0), reads=[B_dt], writes=[B_dt])
                p.dve(C("scalar_tensor_tensor", out=dA[:], in0=dt[:], scalar=-1.0,
                                                       in1=al[:].unsqueeze(1).to_broadcast([128, NCHK, 64]), op0=ALU.mult, op1=ALU.mult),
                      reads=[B_dt, B_al], writes=[B_dA])
                dAf = dA[:].rearrange("p c j -> p (c j)")
                csf = cs[:].rearrange("p c j -> p (c j)")
                totf = tot[:].rearrange("p c j -> p (c j)")
                for i0 in range(0, NQ, 512):
                    n = min(512, NQ - i0)
                    p.pe(C("matmul", PS[0][:, :n], lhsT=masks[:, 0, :], rhs=dAf[:, i0:i0 + n], start=True, stop=True),
                         reads=[B_dA, B_glob], writes=[B_PS[0]])
                    p.dve(C("tensor_copy", out=csf[:, i0:i0 + n], in_=PS[0][:, :n]), reads=[B_PS[0]], writes=[B_cs])
                    p.pe(C("matmul", PS[1][:, :n], lhsT=ones_f[:], rhs=dAf[:, i0:i0 + n], start=True, stop=True),
                         reads=[B_dA, B_glob], writes=[B_PS[1]])
                    p.dve(C("tensor_copy", out=totf[:, i0:i0 + n], in_=PS[1][:, :n]), reads=[B_PS[1]], writes=[B_tot])
                p.act(C("activation", out=cdec[:], in_=tot[:], func=AF.Exp), reads=[B_tot], writes=[B_cdec])
                p.dve(C("tensor_tensor", out=tmpa[:, :, 0:32], in0=tot[:, :, 0:32], in1=cs[:, :, 0:32], op=ALU.subtract),
                      reads=[B_tot, B_cs], writes=[B_tmpa])
                p.dve(C("tensor_tensor", out=tmpa[:, :, 32:64], in0=cs[:, :, 32:64], in1=dA[:, :, 32:64], op=ALU.subtract),
                      reads=[B_cs, B_dA], writes=[B_tmpa])
                p.act(C("activation", out=wst[:], in_=tmpa[:], func=AF.Exp), reads=[B_tmpa], writes=[B_wst])
                p.dve(C("tensor_tensor", out=wst[:], in0=wst[:], in1=dt[:], op=ALU.mult), reads=[B_wst, B_dt], writes=[B_wst])
                p.act(C("activation", out=eoff[:, :, 0:32], in_=cs[:, :, 0:32], func=AF.Exp), reads=[B_cs], writes=[B_eoff])
                p.dve(C("tensor_tensor", out=tmpa[:, :, 32:64], in0=tot[:, :, 32:64], in1=tmpa[:, :, 32:64], op=ALU.subtract),
                      reads=[B_tot, B_tmpa, B_wst], writes=[B_tmpa])
                p.act(C("activation", out=eoff[:, :, 32:64], in_=tmpa[:, :, 32:64], func=AF.Exp), reads=[B_tmpa], writes=[B_eoff])
                p.barrier()
            with ExitStack() as st:
                Sst = [st.enter_context(nc.sbuf_tensor(U("Sst%d" % d), [128, 4, 512], F32)) for d in range(2)]
                B_S = [Buf(), Buf()]
                xr = Ring(p, st, "sx", 2, [128, 2560], BF16)
                xw = Ring(p, st, "sxw", 3, [128, 512], BF16)
                so_ = Ring(p, st, "sso", 3, [128, 4, 512], BF16)
                stmp = st.enter_context(nc.sbuf_tensor(U("stmp"), [128, 512], F32)); B_stmp = Buf()
                for d in range(2):
                    order = list(range(NCHK)) if d == 0 else [1, 0] + list(range(NCHK - 1, 1, -1))
                    p.dve(C("memset", Sst[d][:], 0.0), writes=[B_S[d]])
                    for c in order:
                        sot, B_sot, sso = so_.next()
                        p.act(C("activation", out=sot[:], in_=Sst[d][:], func=AF.Copy), reads=[B_S[d]], writes=[B_sot])
                        p.dma("pool", sin_d[d][c].rearrange("g p n -> p g n"), sot[:], sso, reads=[B_sot], writes=[B_sin[d]])
                        if c == order[-1]:
                            break
                        xt, B_x, sx = xr.next()
                        p.dma("sp", xt[:], xb_tok[c * 128:(c + 1) * 128, :], sx, reads=[B_xbtok], writes=[B_x])
                        for g in range(4):
                            xwt, B_xw, _ = xw.next()
                            p.pool(C("tensor_tensor",
                                out=xwt[:].rearrange("p (h q) -> p h q", q=64), in0=xt[:, g * 512:(g + 1) * 512].rearrange("p (h q) -> p h q", q=64),
                                in1=wst[:, c, d * 32 + g * 8:d * 32 + g * 8 + 8].unsqueeze(2).to_broadcast([128, 8, 64]), op=ALU.mult),
                                reads=[B_x, B_wst], writes=[B_xw])
                            pb = g % 4
                            p.pe(C("matmul",
                                PS[pb][:, :], lhsT=xt[:, 2048 + g * 128:2048 + (g + 1) * 128], rhs=xwt[:], start=True, stop=True),
                                reads=[B_x, B_xw], writes=[B_PS[pb]])
                            p.dve(C("tensor_tensor",
                                out=stmp[:].rearrange("p (h q) -> p h q", q=64), in0=Sst[d][:, g, :].rearrange("p (h q) -> p h q", q=64),
                                in1=cdec[:, c, d * 32 + g * 8:d * 32 + g * 8 + 8].unsqueeze(2).to_broadcast([128, 8, 64]), op=ALU.mult),
                                reads=[B_S[d], B_cdec, B_sot], writes=[B_stmp])
                            p.dve(C("tensor_tensor", out=Sst[d][:, g, :], in0=stmp[:], in1=PS[pb][:, :], op=ALU.add),
                                  reads=[B_stmp, B_PS[pb]], writes=[B_S[d]])
                p.barrier()
            with ExitStack() as st:
                dsk = st.enter_context(nc.sbuf_tensor(U("dsk"), [128, 32], F32)); B_dsk = Buf()
                gn = st.enter_context(nc.sbuf_tensor(U("gn"), [128, DIN], F32))
                p.dma("sp", dsk[:], dskip[:, l, :], s_misc, reads=[B_const], writes=[B_dsk])
                p.dma("sp", gn[:], sng[:, l, :], s_misc, reads=[B_const], writes=[B_dsk])
                xr = Ring(p, st, "yx", 2, [128, 2560], BF16)
                bcr = Ring(p, st, "ybc", 2, [128, 8, 128], BF16)
                sir = Ring(p, st, "ysi", 2, [128, 2, 4, 512], BF16)
                zr = Ring(p, st, "yz", 2, [128, DIN], F32)
                Gm = st.enter_context(nc.sbuf_tensor(U("Gm"), [128, 2, 4, 128], F32)); B_Gm = Buf()
                xd = st.enter_context(nc.sbuf_tensor(U("xd"), [128, 2, DIN], BF16)); B_xd = Buf()
                Lm = [st.enter_context(nc.sbuf_tensor(U("Lm%d" % i), [128, 2, 4, 128], F32)) for i in range(2)]
                B_Lm = [Buf(), Buf()]
                Em = [st.enter_context(nc.sbuf_tensor(U("Em%d" % i), [128, 4, 128], F32)) for i in range(2)]
                B_Em = [Buf(), Buf()]
                Mm = [st.enter_context(nc.sbuf_tensor(U("Mm%d" % i), [128, 2, 4, 128], BF16)) for i in range(2)]
                B_Mm = [Buf(), Buf()]
                ya = st.enter_context(nc.sbuf_tensor(U("ya"), [128, DIN], F32)); B_ya = Buf()
                yt = st.enter_context(nc.sbuf_tensor(U("ytm"), [128, 512], F32)); B_yt = Buf()
                ssq = st.enter_context(nc.sbuf_tensor(U("ssq"), [128, 8], F32)); B_ssq = Buf()
                junk = st.enter_context(nc.sbuf_tensor(U("junk"), [128, 512], F32)); B_junk = Buf()
                yb = st.enter_context(nc.sbuf_tensor(U("yb"), [128, DIN], BF16)); B_yb = Buf()
                yo = Ring(p, st, "yo", 2, [128, 16, 128], BF16)
                chunks = list(range(2, NCHK)) if last else list(range(NCHK))
                it = 0
                for c in chunks:
                    xt, B_x, sx = xr.next()
                    p.dma("sp", xt[:], xb_tok[c * 128:(c + 1) * 128, :], sx, reads=[B_xbtok], writes=[B_x])
                    bct, B_bc, sbc = bcr.next()
                    p.dma("sp", bct[:], bcT[:, :, c * 128:(c + 1) * 128].rearrange("c p t -> p c t"), sbc, reads=[B_bcT], writes=[B_bc])
                    sit, B_si, ssi = sir.next()
                    for d in range(2):
                        p.dma("sp", sit[:, d, :, :], sin_d[d][c].rearrange("g p n -> p g n"), ssi, reads=[B_sin[d]], writes=[B_si])
                    zt, B_zt, sz = zr.next()
                    p.dma("sp", zt[:], z_tok[c * 128:(c + 1) * 128, :], sz, reads=[B_z], writes=[B_zt])
                    for g in range(4):
                        p.pe(C("matmul", PS[6][:, g * 128:(g + 1) * 128], lhsT=bct[:, g, :], rhs=bct[:, 4 + g, :], start=True, stop=True),
                             reads=[B_bc], writes=[B_PS[6]])
                    for d in range(2):
                        p.dve(C("tensor_tensor",
                            out=Gm[:, d, :, :], in0=PS[6][:, :].rearrange("p (g l) -> p g l", l=128),
                            in1=masks[:, d, :].unsqueeze(1).to_broadcast([128, 4, 128]), op=ALU.mult),
                            reads=[B_PS[6], B_glob], writes=[B_Gm])
                    for d in range(2):
                        p.pool(C("tensor_tensor",
                            out=xd[:, d, :].rearrange("p (h q) -> p h q", q=64), in0=xt[:, 0:2048].rearrange("p (h q) -> p h q", q=64),
                            in1=dt[:, c, d * 32:(d + 1) * 32].unsqueeze(2).to_broadcast([128, 32, 64]), op=ALU.mult),
                            reads=[B_x, B_dt], writes=[B_xd])
                    for g in range(4):
                        ydb = g % 2
                        for hq in range(2):
                            k = it % 2
                            it += 1
                            for d in range(2):
                                h0 = d * 32 + g * 8 + hq * 4
                                p.pool(C("tensor_tensor",
                                    out=Lm[k][:, d, :, :], in0=masks[:, 2 + d, :].unsqueeze(1).to_broadcast([128, 4, 128]),
                                    in1=dA[:, c, h0:h0 + 4].unsqueeze(2).to_broadcast([128, 4, 128]), op=ALU.mult),
                                    reads=[B_glob, B_dA], writes=[B_Lm[k]])
                            sb = 2 + k
                            for j in range(4):
                                for d in range(2):
                                    p.pe(C("matmul",
                                        PS[sb][:, j * 128:(j + 1) * 128], lhsT=Lm[k][:, d, j, :], rhs=masks[:, d, :], start=(d == 0), stop=(d == 1)),
                                        reads=[B_Lm[k], B_glob], writes=[B_PS[sb]])
                            p.act(C("activation", out=Em[k][:].rearrange("p j l -> p (j l)"), in_=PS[sb][:, :], func=AF.Exp),
                                  reads=[B_PS[sb]], writes=[B_Em[k]])
                            for d in range(2):
                                p.dve(C("tensor_tensor",
                                    out=Mm[k][:, d, :, :], in0=Em[k][:], in1=Gm[:, d, g, :].unsqueeze(1).to_broadcast([128, 4, 128]), op=ALU.mult),
                                    reads=[B_Em[k], B_Gm], writes=[B_Mm[k]])
                            for j in range(4):
                                hh = g * 8 + hq * 4 + j
                                col = (hq * 4 + j) * 64
                                for d in range(2):
                                    p.pe(C("matmul",
                                        PS[ydb][:, col:col + 64], lhsT=Mm[k][:, d, j, :], rhs=xd[:, d, hh * 64:(hh + 1) * 64],
                                        start=(d == 0), stop=(d == 1)), reads=[B_Mm[k], B_xd], writes=[B_PS[ydb]])
                        for d in range(2):
                            ob = 4 + d
                            p.pe(C("matmul",
                                PS[ob][:, :], lhsT=bct[:, 4 + g, :], rhs=sit[:, d, g, :], start=True, stop=True),
                                reads=[B_bc, B_si], writes=[B_PS[ob]])
                        p.dve(C("tensor_tensor",
                            out=yt[:].rearrange("p (h q) -> p h q", q=64), in0=PS[4][:, :].rearrange("p (h q) -> p h q", q=64),
                            in1=eoff[:, c, g * 8:g * 8 + 8].unsqueeze(2).to_broadcast([128, 8, 64]), op=ALU.mult),
                            reads=[B_PS[4], B_eoff], writes=[B_yt])
                        p.dve(C("tensor_tensor", out=ya[:, g * 512:(g + 1) * 512], in0=yt[:], in1=PS[ydb][:, :], op=ALU.add),
                              reads=[B_yt, B_PS[ydb]], writes=[B_ya])
                        p.dve(C("tensor_tensor",
                            out=yt[:].rearrange("p (h q) -> p h q", q=64), in0=PS[5][:, :].rearrange("p (h q) -> p h q", q=64),
                            in1=eoff[:, c, 32 + g * 8:32 + g * 8 + 8].unsqueeze(2).to_broadcast([128, 8, 64]), op=ALU.mult),
                            reads=[B_PS[5], B_eoff, B_ya], writes=[B_yt])
                        p.dve(C("tensor_tensor", out=ya[:, g * 512:(g + 1) * 512], in0=ya[:, g * 512:(g + 1) * 512], in1=yt[:], op=ALU.add),
                              reads=[B_yt, B_ya], writes=[B_ya])
                        p.dve(C("tensor_tensor",
                            out=yt[:].rearrange("p (h q) -> p h q", q=64), in0=xt[:, g * 512:(g + 1) * 512].rearrange("p (h q) -> p h q", q=64),
                            in1=dsk[:, g * 8:g * 8 + 8].unsqueeze(2).to_broadcast([128, 8, 64]), op=ALU.mult),
                            reads=[B_x, B_dsk, B_ya], writes=[B_yt])
                        p.dve(C("tensor_tensor", out=ya[:, g * 512:(g + 1) * 512], in0=ya[:, g * 512:(g + 1) * 512], in1=yt[:], op=ALU.add),
                              reads=[B_yt, B_ya], writes=[B_ya])
                        p.dve(C("tensor_tensor", out=ya[:, g * 512:(g + 1) * 512], in0=ya[:, g * 512:(g + 1) * 512],
                                                                   in1=zt[:, g * 512:(g + 1) * 512], op=ALU.mult),
                              reads=[B_zt, B_ya], writes=[B_ya])
                        p.act(C("activation", out=junk[:], in_=ya[:, g * 512:(g + 1) * 512], func=AF.Square, accum_out=ssq[:, g:g + 1]),
                              reads=[B_ya], writes=[B_junk, B_ssq])
                        p.act(C("activation", out=ssq[:, 4 + g:5 + g], in_=ssq[:, g:g + 1], func=AF.Sqrt, bias=EPS, scale=1.0 / 512),
                              reads=[B_ssq], writes=[B_ssq])
                        p.dve(C("reciprocal", out=ssq[:, 4 + g:5 + g], in_=ssq[:, 4 + g:5 + g]), reads=[B_ssq], writes=[B_ssq])
                        p.dve(C("scalar_tensor_tensor",
                            out=yb[:, g * 512:(g + 1) * 512], in0=ya[:, g * 512:(g + 1) * 512], scalar=ssq[:, 4 + g:5 + g],
                            in1=gn[:, g * 512:(g + 1) * 512], op0=ALU.mult, op1=ALU.mult),
                            reads=[B_ya, B_ssq, B_dsk], writes=[B_yb])
                    yot, B_yo, syo = yo.next()
                    for q8 in range(2):
                        for j in range(8):
                            cc = q8 * 8 + j
                            p.pe(C("transpose", out=PSB[:, j * 128:(j + 1) * 128], in_=yb[:, cc * 128:(cc + 1) * 128], identity=ident_b[:]),
                                 reads=[B_yb, B_glob], writes=[B_PSB])
                        p.act(C("activation", out=yot[:, q8 * 8:(q8 + 1) * 8, :], in_=PSB[:, :].rearrange("p (a b) -> p a b", b=128), func=AF.Copy),
                              reads=[B_PSB], writes=[B_yo])
                    p.dma("pool", yT[:, :, c * 128:(c + 1) * 128].rearrange("c p t -> p c t"), yot[:], syo, reads=[B_yo], writes=[B_yT])
                p.barrier()

    def phase_attn(l, last):
        with ExitStack() as st:
            Kt = st.enter_context(nc.sbuf_tensor(U("Kt"), [128, 2, TT], BF16)); B_K = Buf()
            Vt = st.enter_context(nc.sbuf_tensor(U("Vt"), [128, NCHK, 256], BF16)); B_V = Buf()
            p.dma("sp", Kt[:], kT.rearrange("c p t -> p c t"), s_misc, reads=[B_k], writes=[B_K])
            p.dma("sp", Vt[:], v_tok.rearrange("(a p) j -> p a j", p=128), s_misc, reads=[B_v], writes=[B_V])
            qr = Ring(p, st, "aq", 2, [128, 8, 512], BF16)
            pr = Ring(p, st, "ap", 4, [128, 512], BF16)
            ao = Ring(p, st, "ao", 2, [128, 8, 512], BF16)
            rd = st.enter_context(nc.sbuf_tensor(U("rd"), [128, 512], F32)); B_rd = Buf()
            sel = list(range(1, NT)) if last else list(range(NT))
            sidx = 0
            for ti in sel:
                t0, T, wh = tiles[ti]
                kts = [0, 1] if wh == 1 else list(range(NCHK))
                nk = len(kts)
                qt, B_qt, sq_ = qr.next()
                p.dma("sp", qt[:, :, :T], qT[:, :, t0:t0 + T].rearrange("c p t -> p c t"), sq_, reads=[B_q], writes=[B_qt])
                aot, B_ao, sao = ao.next()
                steps = [(hd, i, kt) for hd in range(8) for i, kt in enumerate(kts)]
                pend = None

                def do_pv(st_):
                    hd, i, kt, pt, B_pt = st_
                    kv = hd // 4
                    ob = 3 + (hd % 2)
                    db_ = 5 + (hd % 2)
                    p.pe(C("matmul", PS[ob][:, :T], lhsT=Vt[:, kt, kv * 128:(kv + 1) * 128], rhs=pt[:, :T],
                           start=(i == 0), stop=(i == nk - 1)), reads=[B_V, B_pt], writes=[B_PS[ob]])
                    p.pe(C("matmul", PS[db_][:, :T], lhsT=ones_b[:], rhs=pt[:, :T],
                           start=(i == 0), stop=(i == nk - 1)), reads=[B_glob, B_pt], writes=[B_PS[db_]])
                    if i == nk - 1:
                        p.dve(C("reciprocal", out=rd[:, :T], in_=PS[db_][:, :T]), reads=[B_PS[db_]], writes=[B_rd])
                        p.dve(C("tensor_tensor", out=aot[:, hd, :T], in0=PS[ob][:, :T], in1=rd[:, :T], op=ALU.mult),
                              reads=[B_PS[ob], B_rd], writes=[B_ao])

                for (hd, i, kt) in steps:
                    kv = hd // 4
                    sb = sidx % 3
                    sidx += 1
                    p.pe(C("matmul", PS[sb][:, :T], lhsT=Kt[:, kv, kt * 128:(kt + 1) * 128], rhs=qt[:, hd, :T], start=True, stop=True),
                         reads=[B_K, B_qt], writes=[B_PS[sb]])
                    pt, B_pt, _ = pr.next()
                    p.act(C("activation", out=pt[:, :T], in_=PS[sb][:, :T], func=AF.Exp, scale=ATT_SCALE),
                          reads=[B_PS[sb]], writes=[B_pt])
                    if pend is not None:
                        do_pv(pend)
                    pend = (hd, i, kt, pt, B_pt)
                do_pv(pend)
                p.dma("pool", attT[:, :, t0:t0 + T].rearrange("c p t -> p c t"), aot[:, :, :T], sao, reads=[B_ao], writes=[B_att])
            p.barrier()

    def phase_merge(l, last):
        with ExitStack() as st:
            Wso = st.enter_context(nc.sbuf_tensor(U("Wso"), [128, 16, D], BF16)); B_W = Buf()
            Wao = st.enter_context(nc.sbuf_tensor(U("Wao"), [128, KC, D], BF16))
            Wo = st.enter_context(nc.sbuf_tensor(U("Wo"), [128, KC, D], BF16))
            sW = sem_cached("wA")
            load_w(Wso[:], B_W, w_so[l].rearrange("(k p) n -> p k n", p=128), sW)
            load_w(Wao[:], B_W, w_ao[l].rearrange("(k p) n -> p k n", p=128), sW)
            load_w(Wo[:], B_W, w_o[l].rearrange("(k p) n -> p k n", p=128), sW)
            xr = Ring(p, st, "mx", 2, [128, KC, 512], F32)
            yr = Ring(p, st, "my", 2, [128, 16, 512], BF16)
            ar = Ring(p, st, "ma", 2, [128, KC, 512], BF16)
            gr = Ring(p, st, "mg", 3, [128, 2, 512], F32)
            m = st.enter_context(nc.sbuf_tensor(U("mm"), [128, KC, 512], BF16)); B_m = Buf()
            t1 = st.enter_context(nc.sbuf_tensor(U("mt1"), [128, 512], F32)); B_t1 = Buf()
            t2 = st.enter_context(nc.sbuf_tensor(U("mt2"), [128, 512], F32)); B_t2 = Buf()
            sel = list(range(1, NT)) if last else list(range(NT))
            for ti in sel:
                t0, T, wh = tiles[ti]
                xt, B_x, sx = xr.next()
                p.dma("sp", xt[:, :, :T], xT[:, :, t0:t0 + T].rearrange("c p t -> p c t"), sx, reads=[B_xT[ti]], writes=[B_x])
                yt, B_y, sy = yr.next()
                p.dma("sp", yt[:, :, :T], yT[:, :, t0:t0 + T].rearrange("c p t -> p c t"), sy, reads=[B_yT], writes=[B_y])
                at, B_a, sa_ = ar.next()
                p.dma("sp", at[:, :, :T], attT[:, :, t0:t0 + T].rearrange("c p t -> p c t"), sa_, reads=[B_att], writes=[B_a])
                for n in range(KC):
                    gt_, B_gt, sg = gr.next()
                    p.dma("sp", gt_[:, 0, :T], gT[n, :, t0:t0 + T], sg, reads=[B_g], writes=[B_gt])
                    p.dma("sp", gt_[:, 1, :T], gT[8 + n, :, t0:t0 + T], sg, reads=[B_g], writes=[B_gt])
                    pa, pb = (0, 1) if n % 2 == 0 else (2, 3)
                    for kc in range(16):
                        p.pe(C("matmul",
                            PS[pa][:, :T], lhsT=Wso[:, kc, n * 128:(n + 1) * 128], rhs=yt[:, kc, :T], start=(kc == 0), stop=(kc == 15)),
                            reads=[B_W, B_y], writes=[B_PS[pa]])
                    for kc in range(KC):
                        p.pe(C("matmul",
                            PS[pb][:, :T], lhsT=Wao[:, kc, n * 128:(n + 1) * 128], rhs=at[:, kc, :T], start=(kc == 0), stop=(kc == KC - 1)),
                            reads=[B_W, B_a], writes=[B_PS[pb]])
                    p.dve(C("tensor_tensor", out=t1[:, :T], in0=PS[pa][:, :T], in1=gt_[:, 0, :T], op=ALU.mult),
                          reads=[B_PS[pa], B_gt], writes=[B_t1])
                    p.dve(C("tensor_tensor", out=t2[:, :T], in0=PS[pb][:, :T], in1=gt_[:, 1, :T], op=ALU.mult),
                          reads=[B_PS[pb], B_gt], writes=[B_t2])
                    p.dve(C("tensor_tensor", out=m[:, n, :T], in0=t1[:, :T], in1=t2[:, :T], op=ALU.add),
                          reads=[B_t1, B_t2], writes=[B_m])
                for n in range(KC):
                    pb = 4 + n % 2
                    for kc in range(KC):
                        p.pe(C("matmul",
                            PS[pb][:, :T], lhsT=Wo[:, kc, n * 128:(n + 1) * 128], rhs=m[:, kc, :T], start=(kc == 0), stop=(kc == KC - 1)),
                            reads=[B_W, B_m], writes=[B_PS[pb]])
                    p.dve(C("scalar_tensor_tensor",
                        out=xt[:, n, :T], in0=PS[pb][:, :T], scalar=HG[:, l, 1, n, wh:wh + 1], in1=xt[:, n, :T],
                        op0=ALU.mult, op1=ALU.add), reads=[B_PS[pb], B_x, B_MOD], writes=[B_x])
                p.dma("pool", xT[:, :, t0:t0 + T].rearrange("c p t -> p c t"), xt[:, :, :T], sx, reads=[B_x], writes=[B_xT[ti]])
            p.barrier()

    phase_ada()
    for l in range(L):
        last = last_flags[l]
        phase_ffn(l, 0, 0, list(range(NT)), False)
        phase_inproj1(l)
        phase_inproj2(l)
        phase_inproj3(l)
        phase_conv(l)
        phase_ssd(l, last)
        phase_attn(l, last)
        phase_merge(l, last)
        fin = (l == L - 1)
        phase_ffn(l, 1, 2, list(range(1, NT)) if last else list(range(NT)), fin)
    p.finalize()
    return nc, p


def rope_tables(S):
    GRID_W = 64
    t = np.arange(S)
    row = (t // GRID_W).astype(np.float32)
    col = (t % GRID_W).astype(np.float32)
    inv = (10000.0 ** (-(np.arange(0, 64, 2, dtype=np.float32) / 64.0))).astype(np.float32)
    ar = (row[None, :] * inv[:, None]).astype(np.float32)
    ac = (col[None, :] * inv[:, None]).astype(np.float32)
    cos = np.concatenate([np.cos(ar), np.cos(ar), np.cos(ac), np.cos(ac)], 0)
    sin = np.concatenate([-np.sin(ar), np.sin(ar), -np.sin(ac), np.sin(ac)], 0)
    cosT = np.concatenate([np.ones((128, CTX), np.float32), cos.astype(np.float32)], 1)
    sinT = np.concatenate([np.zeros((128, CTX), np.float32), sin.astype(np.float32)], 1)
    return np.ascontiguousarray(cosT), np.ascontiguousarray(sinT)


def fm(v):
    sh = v.shape
    n = sh[-1] // 128
    r = v.reshape(*sh[:-1], n, 128)
    return np.ascontiguousarray(np.moveaxis(r, -1, 0))


def rep(v):
    return np.ascontiguousarray(np.broadcast_to(v[None], (128,) + v.shape))


def host_prep(inp, S, L):
    f = np.float32
    shared = {}
    shared["bada"] = fm(inp["b_ada"][:L].astype(f))
    shared["ng"] = fm(inp["norm_g"][:L].astype(f))
    shared["w_ada"] = np.ascontiguousarray(inp["w_ada"][:L])
    for k in ("ffn1_w13", "ffn1_w2", "ffn2_w13", "ffn2_w2", "w_in", "w_ssd_out", "w_attn_out", "w_out"):
        shared[k] = np.ascontiguousarray(inp[k][:L])
    shared["convw"] = fm(inp["conv_w"][:L].astype(f))
    shared["convb"] = fm(inp["conv_b"][:L].astype(f))
    shared["alog"] = rep(inp["a_log"][:L].reshape(L, 64).astype(f))
    shared["dtb"] = rep(inp["dt_bias"][:L].reshape(L, 64).astype(f))
    shared["dskip"] = rep(inp["d_skip"][:L].astype(f))
    shared["sng"] = rep(inp["ssd_norm_g"][:L].astype(f))
    g = inp["qk_norm_g"][:L].astype(f)
    gp = g.reshape(L, 2, 2, 2, 32)[:, :, :, ::-1, :].reshape(L, 2, 128)
    shared["qkg"] = np.ascontiguousarray(np.concatenate([g, gp], 1).transpose(2, 0, 1))
    cosT, sinT = rope_tables(S)
    shared["cosT"] = cosT
    shared["sinT"] = sinT
    shared["ident"] = np.eye(128, dtype=f)
    k = np.arange(128)[:, None]
    j = np.arange(128)[None, :]
    shared["masks"] = np.ascontiguousarray(np.stack([(k <= j), (k >= j), (k > j), (k < j)], 1).astype(f))
    maps = []
    B = inp["x"].shape[0]
    for b in range(B):
        m = dict(shared)
        xa = np.concatenate([inp["ctx"][b], inp["x"][b][:S]], 0).astype(f)
        m["xT0"] = np.ascontiguousarray(xa.T.reshape(KC, 128, CTX + S))
        cc = np.stack([inp["c"][b], inp["c_ctx"]], -1).astype(f)
        m["csT"] = np.ascontiguousarray(cc.reshape(KC, 128, 2).transpose(1, 0, 2))
        maps.append(m)
    return maps


_CACHE = {}


def kernel(**inputs):
    S, L = 4096, 4
    inp = {k: np.asarray(v) for k, v in inputs.items()}
    if "nc" not in _CACHE:
        _CACHE["nc"] = build_program(S, L)[0]
    nc = _CACHE["nc"]
    maps = host_prep(inp, S, L)
    res = run_bass_kernel_spmd(nc, maps, core_ids=list(range(8)))
    outs = []
    for r in res.results:
        o = np.asarray(r["outT"])
        outs.append(np.ascontiguousarray(o.reshape(D, S).T))
    return np.stack(outs, 0).astype(np.float32)
```

```python
from contextlib import ExitStack
import numpy as np
import concourse.bass as bass
import concourse.mybir as mybir
from concourse.bass_utils import run_bass_kernel_spmd

F32 = mybir.dt.float32
BF16 = mybir.dt.bfloat16
AF = mybir.ActivationFunctionType
ALU = mybir.AluOpType

ENGS = ("pe", "act", "dve", "pool", "sp")
EPS = 1e-6


class Buf:
    __slots__ = ("name", "w", "r", "dram")

    def __init__(self, name="", dram=False):
        self.name = name
        self.w = {}
        self.r = {}
        self.dram = dram


class Op:
    __slots__ = ("eng", "emit", "deps", "signal", "sigcount", "dsem", "dcount", "key")

    def __init__(self, eng, emit):
        self.eng = eng
        self.emit = emit
        self.deps = []
        self.signal = False
        self.sigcount = 0
        self.dsem = None
        self.dcount = 0
        self.key = eng


class Prog:
    def __init__(self, nc):
        self.nc = nc
        self.es = ExitStack()
        self.streams = {e: [] for e in ENGS}
        self.esem = {e: self.es.enter_context(nc.semaphore("es_" + e)) for e in ENGS}
        self.dma_counts = {}
        self.last_dma = {}
        self.last_compute = {}

    def sem(self, name):
        s = self.es.enter_context(self.nc.semaphore(name))
        self.dma_counts[id(s)] = 0
        return s

    def _rec(self, eng, emit, reads, writes, dsem=None):
        op = Op(eng, emit)
        isdma = dsem is not None
        if isdma:
            op.dsem = dsem
            self.dma_counts[id(dsem)] += 16
            op.dcount = self.dma_counts[id(dsem)]
            op.key = ("dma", id(dsem))
            self.last_dma[id(dsem)] = op
        else:
            self.last_compute[eng] = op
        key = op.key
        deps = {}
        for b in reads:
            for k, d in b.w.items():
                deps[id(d)] = d
        for b in writes:
            for k, d in b.w.items():
                if isdma:
                    if b.dram and d.dsem is not None:
                        continue
                    deps[id(d)] = d
                elif k != key:
                    deps[id(d)] = d
            for k, d in b.r.items():
                if isdma or k != key:
                    deps[id(d)] = d
        for d in deps.values():
            d.signal = True
            op.deps.append((d, 0))
        for b in reads:
            b.r[key] = op
        for b in writes:
            b.w[key] = op
        self.streams[eng].append(op)
        return op

    def pe(self, emit, reads=(), writes=()):
        return self._rec("pe", emit, reads, writes)

    def act(self, emit, reads=(), writes=()):
        return self._rec("act", emit, reads, writes)

    def dve(self, emit, reads=(), writes=()):
        return self._rec("dve", emit, reads, writes)

    def pool(self, emit, reads=(), writes=()):
        return self._rec("pool", emit, reads, writes)

    def dma(self, queue, out, in_, sem, reads=(), writes=()):
        def emit(e, out=out, in_=in_):
            return e.dma_start(out=out, in_=in_)
        return self._rec(queue, emit, reads, writes, dsem=sem)

    def barrier(self):
        r1 = []
        for e in ENGS:
            op = Op(e, lambda eng: eng.nop())
            deps = []
            if e in self.last_compute:
                deps.append(self.last_compute[e])
            if e == "sp":
                deps.extend(self.last_dma.values())
            for d in deps:
                d.signal = True
                op.deps.append((d, 0))
            self.streams[e].append(op)
            self.last_compute[e] = op
            r1.append(op)
        for e in ENGS:
            op = Op(e, lambda eng: eng.nop())
            for d in r1:
                if d.eng != e:
                    d.signal = True
                    op.deps.append((d, 0))
            self.streams[e].append(op)
            self.last_compute[e] = op

    def finalize(self):
        nc = self.nc
        for e in ENGS:
            c = 0
            for op in self.streams[e]:
                if op.dsem is None and op.signal:
                    c += 1
                    op.sigcount = c
        self.stats = {}
        with nc.Block() as block:
            def run(ename, eng):
                seen = {}
                nwait = 0
                for op in self.streams[ename]:
                    waits = {}
                    for d, cnt in op.deps:
                        if d.dsem is not None:
                            s, v = d.dsem, max(cnt, d.dcount)
                        else:
                            s, v = self.esem[d.eng], d.sigcount
                        k = id(s)
                        if seen.get(k, 0) >= v:
                            continue
                        if k not in waits or waits[k][1] < v:
                            waits[k] = (s, v)
                    for k, (s, v) in waits.items():
                        eng.wait_ge(s, v)
                        seen[k] = v
                        nwait += 1
                    ins = op.emit(eng)
                    if op.dsem is not None:
                        ins.then_inc(op.dsem, 16)
                    elif op.signal:
                        ins.then_inc(self.esem[ename], 1)
                self.stats[ename] = (len(self.streams[ename]), nwait)

            @block.tensor
            def _(eng):
                run("pe", eng)

            @block.scalar
            def _(eng):
                run("act", eng)

            @block.vector
            def _(eng):
                run("dve", eng)

            @block.gpsimd
            def _(eng):
                run("pool", eng)

            @block.sync
            def _(eng):
                run("sp", eng)
        self.es.close()


D = 1024
KC = 8
DFF = 2816
HC = 22
NMOD = 9
CTX = 256
DIN = 2048
CONV = 3072
INDIM = 8768
C_Z, C_XBC, C_DT, C_Q, C_K, C_V, C_GS, C_GA = 0, 2048, 5120, 5184, 6208, 6464, 6720, 7744
ATT_SCALE = 128 ** -0.5


_UC = [0]


def U(name):
    _UC[0] += 1
    return "%s_%d" % (name, _UC[0])


def C(name, *args, **kw):
    def emit(e):
        return getattr(e, name)(*args, **kw)
    return emit


class Ring:
    def __init__(self, p, stack, name, n, shape, dt):
        self.t = [stack.enter_context(p.nc.sbuf_tensor(U("%s%d" % (name, i)), list(shape), dt)) for i in range(n)]
        self.b = [Buf("%s%d" % (name, i)) for i in range(n)]
        self.s = [p.sem_cached("%s%d" % (name, i)) for i in range(n)]
        self.n = n
        self.i = 0

    def next(self):
        k = self.i % self.n
        self.i += 1
        return self.t[k], self.b[k], self.s[k]


def build_program(S, L, last_flags=None):
    if last_flags is None:
        last_flags = [i == L - 1 for i in range(L)]
    TT = CTX + S
    NCHK = TT // 128
    tiles = [(0, CTX, 1)] + [(CTX + 512 * i, 512, 0) for i in range(S // 512)]
    NT = len(tiles)

    nc = bass.Bass("TRN2", target_bir_lowering=False)
    p = Prog(nc)
    semcache = {}

    def sem_cached(name):
        if name not in semcache:
            semcache[name] = p.sem(name)
        return semcache[name]
    p.sem_cached = sem_cached

    def din(name, shape, dt=F32):
        return nc.dram_tensor(name, list(shape), dt, kind="ExternalInput").ap()

    def dscr(name, shape, dt):
        return nc.dram_tensor(name, list(shape), dt).ap()

    xT0 = din("xT0", [KC, 128, TT])
    csT = din("csT", [128, KC, 2])
    bada = din("bada", [128, L, 72])
    ng = din("ng", [128, L, 3, KC])
    w_ada = din("w_ada", [L, D, NMOD * D])
    w13 = [din("ffn1_w13", [L, D, 2 * DFF]), din("ffn2_w13", [L, D, 2 * DFF])]
    w2 = [din("ffn1_w2", [L, DFF, D]), din("ffn2_w2", [L, DFF, D])]
    w_in = din("w_in", [L, D, INDIM])
    convw = din("convw", [128, L, 3, 24])
    convb = din("convb", [128, L, 24])
    alog = din("alog", [128, L, 64])
    dtb = din("dtb", [128, L, 64])
    dskip = din("dskip", [128, L, 32])
    sng = din("sng", [128, L, DIN])
    w_so = din("w_ssd_out", [L, DIN, D])
    qkg = din("qkg", [128, L, 4])
    w_ao = din("w_attn_out", [L, D, D])
    w_o = din("w_out", [L, D, D])
    cosT = din("cosT", [128, TT])
    sinT = din("sinT", [128, TT])
    identd = din("ident", [128, 128])
    masksd = din("masks", [128, 4, 128])
    negindd = din("negind", [12, 512])
    negmaskd = din("negmask", [128, 2, 512])
    outT = nc.dram_tensor("outT", [KC, 128, S], F32, kind="ExternalOutput").ap()

    xT = dscr("xT", [KC, 128, TT], F32)
    uT = dscr("uT", [HC, 128, TT], BF16)
    z_tok = dscr("z_tok", [TT, DIN], F32)
    gT = dscr("gT", [16, 128, TT], F32)
    xbcp = dscr("xbcp", [24, 128, TT], F32)
    dtraw = dscr("dtraw", [TT, 64], F32)
    xsT = dscr("xsT", [16, 128, TT], BF16)
    bcT = dscr("bcT", [8, 128, TT], BF16)
    xb_tok = dscr("xb_tok", [TT, 2560], BF16)
    qT = dscr("qT", [8, 128, TT], BF16)
    kT = dscr("kT", [2, 128, TT], BF16)
    v_tok = dscr("v_tok", [TT, 256], BF16)
    sin_d = [dscr("sinf", [NCHK, 4, 128, 512], BF16), dscr("sinb", [NCHK, 4, 128, 512], BF16)]
    yT = dscr("yT", [16, 128, TT], BF16)
    csD = dscr("csD", [NCHK, 64, 3, 128], BF16)
    attT = dscr("attT", [8, 128, TT], BF16)

    B_xT = [Buf("xT%d" % i, True) for i in range(NT)]
    B_x0 = Buf("xT0", True)
    B_uT = [Buf("uT%d" % i, True) for i in range(NT)]
    B_z = Buf("z", True); B_g = Buf("g", True); B_xbcp = Buf("xbcp", True); B_dtraw = Buf("dtraw", True)
    B_xsT = Buf("xsT", True); B_bcT = Buf("bcT", True); B_xbtok = Buf("xbtok", True)
    B_q = Buf("q", True); B_k = Buf("k", True); B_v = Buf("v", True)
    B_sin = [Buf("sinf", True), Buf("sinb", True)]
    B_csD = Buf("csD", True)
    B_yT = Buf("yT", True); B_att = Buf("att", True); B_out = Buf("out", True)
    B_const = Buf("const", True)

    gs = p.es
    MOD = gs.enter_context(nc.sbuf_tensor(U("MOD"), [128, L, 72, 2], F32)); B_MOD = Buf("MOD")
    AS = gs.enter_context(nc.sbuf_tensor(U("AS"), [128, L, 3, KC, 2], F32))
    HG = gs.enter_context(nc.sbuf_tensor(U("HG"), [128, L, 3, KC, 2], F32))
    NG = gs.enter_context(nc.sbuf_tensor(U("NG"), [128, L, 3, KC], F32))
    ones_b = gs.enter_context(nc.sbuf_tensor(U("ones_b"), [128, 128], BF16))
    ones_f = gs.enter_context(nc.sbuf_tensor(U("ones_f"), [128, 128], F32))
    ident_b = gs.enter_context(nc.sbuf_tensor(U("ident_b"), [128, 128], BF16))
    masks = gs.enter_context(nc.sbuf_tensor(U("masks_s"), [128, 4, 128], F32))
    negind = gs.enter_context(nc.sbuf_tensor(U("negind_s"), [12, 512], BF16))
    negmask = gs.enter_context(nc.sbuf_tensor(U("negmask_s"), [128, 2, 512], BF16))
    B_glob = Buf("glob")
    PS = [gs.enter_context(nc.psum_tensor("ps%d" % i, [128, 512], F32)) for i in range(7)]
    B_PS = [Buf("ps%d" % i) for i in range(7)]
    PSB = gs.enter_context(nc.psum_tensor("psb", [128, 1024], BF16)); B_PSB = Buf("psb")
    s_misc = sem_cached("misc")

    def MODap(l, j, ch, which):
        return MOD[:, l, j * 8 + ch, which:which + 1]

    def phase_ada():
        with ExitStack() as st:
            cs = st.enter_context(nc.sbuf_tensor(U("cs"), [128, KC, 2], F32)); B_cs = Buf()
            cs2 = st.enter_context(nc.sbuf_tensor(U("cs2"), [128, KC, 2], F32)); B_cs2 = Buf()
            bd = st.enter_context(nc.sbuf_tensor(U("bd"), [128, L, 72], F32)); B_bd = Buf()
            p.dma("sp", cs[:], csT, s_misc, reads=[B_const], writes=[B_cs])
            p.dma("sp", bd[:], bada, s_misc, reads=[B_const], writes=[B_bd])
            p.dma("sp", NG[:], ng, s_misc, reads=[B_const], writes=[B_glob])
            p.dma("sp", masks[:], masksd, s_misc, reads=[B_const], writes=[B_glob])
            p.dma("pool", ident_b[:], identd, s_misc, reads=[B_const], writes=[B_glob])
            p.dma("pool", negind[:], negindd, s_misc, reads=[B_const], writes=[B_glob])
            p.dma("pool", negmask[:], negmaskd, s_misc, reads=[B_const], writes=[B_glob])
            p.dve(C("memset", ones_b[:], 1.0), writes=[B_glob])
            p.dve(C("memset", ones_f[:], 1.0), writes=[B_glob])
            p.act(C("activation", out=cs2[:], in_=cs[:], func=AF.Silu), reads=[B_cs], writes=[B_cs2])
            ring = Ring(p, st, "wada", 2, [128, KC, 1024], F32)
            for l in range(L):
                wv = w_ada[l].rearrange("(k p) n -> p k n", p=128)
                for j in range(NMOD):
                    wt, wb, ws = ring.next()
                    p.dma("sp", wt[:], wv[:, :, j * 1024:(j + 1) * 1024], ws, reads=[B_const], writes=[wb])
                    for ch in range(8):
                        col = (j * 8 + ch) * 2
                        for kc in range(KC):
                            p.pe(C("matmul",
                                PS[0][:, col:col + 2], lhsT=wt[:, kc, ch * 128:(ch + 1) * 128], rhs=cs2[:, kc, :],
                                start=(kc == 0), stop=(kc == KC - 1)), reads=[wb, B_cs2], writes=[B_PS[0]])
                p.dve(C("tensor_tensor",
                    out=MOD[:, l, :, :], in0=PS[0][:, 0:144].rearrange("p (a b) -> p a b", b=2),
                    in1=bd[:, l, :].unsqueeze(2).to_broadcast([128, 72, 2]), op=ALU.add),
                    reads=[B_PS[0], B_bd], writes=[B_MOD])
                for s in range(3):
                    sc = MOD[:, l, (3 * s + 1) * 8:(3 * s + 2) * 8, :]
                    gt = MOD[:, l, (3 * s + 2) * 8:(3 * s + 3) * 8, :]
                    p.dve(C("scalar_tensor_tensor",
                        out=AS[:, l, s, :, :], in0=sc, scalar=1.0,
                        in1=NG[:, l, s, :].unsqueeze(2).to_broadcast([128, KC, 2]), op0=ALU.add, op1=ALU.mult),
                        reads=[B_MOD, B_glob], writes=[B_MOD])
                    p.dve(C("tensor_scalar",
                        out=HG[:, l, s, :, :], in0=gt, scalar1=(1.0 if s == 1 else 0.5), scalar2=None, op0=ALU.mult),
                        reads=[B_MOD], writes=[B_MOD])
            for ti, (t0, T, wh) in enumerate(tiles):
                p.dma("sp", xT[:, :, t0:t0 + T], xT0[:, :, t0:t0 + T], s_misc, reads=[B_x0], writes=[B_xT[ti]])
            p.barrier()

    def norm_mod(xt, B_x, h, B_h, sqr, rs, B_rs, tmpr, T, l, s, wh, psb):
        for c in range(KC):
            sq, B_sq, _ = sqr.next()
            p.act(C("activation", out=sq[:, :T], in_=xt[:, c, :T], func=AF.Square),
                  reads=[B_x], writes=[B_sq])
            p.pe(C("matmul", PS[psb][:, :T], lhsT=ones_b[:], rhs=sq[:, :T], start=(c == 0), stop=(c == KC - 1)),
                 reads=[B_sq, B_glob], writes=[B_PS[psb]])
        p.act(C("activation", out=rs[:, :T], in_=PS[psb][:, :T], func=AF.Sqrt, bias=EPS, scale=1.0 / D),
              reads=[B_PS[psb]], writes=[B_rs])
        p.dve(C("reciprocal", out=rs[:, :T], in_=rs[:, :T]), reads=[B_rs], writes=[B_rs])
        for c in range(KC):
            tmp, B_tmp, _ = tmpr.next()
            p.dve(C("scalar_tensor_tensor",
                out=tmp[:, :T], in0=xt[:, c, :T], scalar=AS[:, l, s, c, wh:wh + 1], in1=rs[:, :T],
                op0=ALU.mult, op1=ALU.mult), reads=[B_x, B_rs, B_MOD], writes=[B_tmp])
            p.act(C("activation", out=h[:, c, :T], in_=tmp[:, :T], func=AF.Identity,
                                              bias=MODap(l, 3 * s, c, wh), scale=1.0),
                  reads=[B_tmp, B_MOD], writes=[B_h])

    def load_w(dst, B_dst, src_ap, sem):
        p.dma("pool", dst, src_ap, sem, reads=[B_const], writes=[B_dst])

    def phase_ffn(l, f, s, tsel, final):
        with ExitStack() as st:
            W = st.enter_context(nc.sbuf_tensor(U("W13"), [128, KC, 2 * DFF], BF16)); B_W = Buf()
            sW = sem_cached("wA")
            wv = w13[f][l].rearrange("(k p) n -> p k n", p=128)
            for kc in range(KC):
                load_w(W[:, kc, :], B_W, wv[:, kc, :], sW)
            xr = Ring(p, st, "xa", 2, [128, KC, 512], F32)
            hr = Ring(p, st, "hA", 2, [128, KC, 512], BF16)
            sqr = Ring(p, st, "sqr", 2, [128, 512], BF16)
            tmpr = Ring(p, st, "tmpr", 2, [128, 512], F32)
            rs = st.enter_context(nc.sbuf_tensor(U("rs"), [128, 512], F32)); B_rs = Buf()
            sa = [st.enter_context(nc.sbuf_tensor(U("sa%d" % i), [128, 512], F32)) for i in range(2)]
            B_sa = [Buf(), Buf()]
            ur = Ring(p, st, "ua", 2, [128, HC, 512], BF16)
            for ti in tsel:
                t0, T, wh = tiles[ti]
                xt, B_x, sx = xr.next()
                p.dma("sp", xt[:, :, :T], xT[:, :, t0:t0 + T].rearrange("c p t -> p c t"), sx, reads=[B_xT[ti]], writes=[B_x])
                h, B_h, _ = hr.next()
                norm_mod(xt, B_x, h, B_h, sqr, rs, B_rs, tmpr, T, l, s, wh, 6)
                ut, B_u, su = ur.next()
                for hc in range(HC):
                    pa, pg = (0, 1) if hc % 2 == 0 else (2, 3)
                    for kc in range(KC):
                        p.pe(C("matmul",
                            PS[pa][:, :T], lhsT=W[:, kc, hc * 128:(hc + 1) * 128], rhs=h[:, kc, :T],
                            start=(kc == 0), stop=(kc == KC - 1)), reads=[B_W, B_h], writes=[B_PS[pa]])
                    for kc in range(KC):
                        p.pe(C("matmul",
                            PS[pg][:, :T], lhsT=W[:, kc, DFF + hc * 128:DFF + (hc + 1) * 128], rhs=h[:, kc, :T],
                            start=(kc == 0), stop=(kc == KC - 1)), reads=[B_W, B_h], writes=[B_PS[pg]])
                    k2 = hc % 2
                    p.act(C("activation", out=sa[k2][:, :T], in_=PS[pa][:, :T], func=AF.Silu),
                          reads=[B_PS[pa]], writes=[B_sa[k2]])
                    p.dve(C("tensor_tensor",
                        out=ut[:, hc, :T], in0=sa[k2][:, :T], in1=PS[pg][:, :T], op=ALU.mult),
                        reads=[B_sa[k2], B_PS[pg]], writes=[B_u])
                p.dma("pool", uT[:, :, t0:t0 + T].rearrange("c p t -> p c t"), ut[:, :, :T], su, reads=[B_u], writes=[B_uT[ti]])
            p.barrier()
        with ExitStack() as st:
            W = st.enter_context(nc.sbuf_tensor(U("W2"), [128, HC, D], BF16)); B_W = Buf()
            sW = sem_cached("wA")
            load_w(W[:], B_W, w2[f][l].rearrange("(k p) n -> p k n", p=128), sW)
            xr = Ring(p, st, "xb", 2, [128, KC, 512], F32)
            ur = Ring(p, st, "ub", 2, [128, HC, 512], BF16)
            for ti in tsel:
                t0, T, wh = tiles[ti]
                xt, B_x, sx = xr.next()
                ut, B_u, su = ur.next()
                p.dma("sp", ut[:, :, :T], uT[:, :, t0:t0 + T].rearrange("c p t -> p c t"), su, reads=[B_uT[ti]], writes=[B_u])
                p.dma("sp", xt[:, :, :T], xT[:, :, t0:t0 + T].rearrange("c p t -> p c t"), sx, reads=[B_xT[ti]], writes=[B_x])
                for n in range(KC):
                    pb = n % 4
                    for kc in range(HC):
                        p.pe(C("matmul",
                            PS[pb][:, :T], lhsT=W[:, kc, n * 128:(n + 1) * 128], rhs=ut[:, kc, :T],
                            start=(kc == 0), stop=(kc == HC - 1)), reads=[B_W, B_u], writes=[B_PS[pb]])
                    p.dve(C("scalar_tensor_tensor",
                        out=xt[:, n, :T], in0=PS[pb][:, :T], scalar=HG[:, l, s, n, wh:wh + 1], in1=xt[:, n, :T],
                        op0=ALU.mult, op1=ALU.add), reads=[B_PS[pb], B_x, B_MOD], writes=[B_x])
                if final and wh == 0:
                    p.dma("pool", outT[:, :, t0 - CTX:t0 - CTX + T].rearrange("c p t -> p c t"), xt[:, :, :T], sx,
                          reads=[B_x], writes=[B_out])
                else:
                    p.dma("pool", xT[:, :, t0:t0 + T].rearrange("c p t -> p c t"), xt[:, :, :T], sx,
                          reads=[B_x], writes=[B_xT[ti]])
            p.barrier()

    def inproj_common(st):
        xr = Ring(p, st, "xc", 2, [128, KC, 512], F32)
        hr = Ring(p, st, "hB", 2, [128, KC, 512], BF16)
        sqr = Ring(p, st, "sqr", 2, [128, 512], BF16)
        tmpr = Ring(p, st, "tmpr", 2, [128, 512], F32)
        rs = st.enter_context(nc.sbuf_tensor(U("rs"), [128, 512], F32)); B_rs = Buf()

        def prep(ti, l):
            t0, T, wh = tiles[ti]
            xt, B_x, sx = xr.next()
            p.dma("sp", xt[:, :, :T], xT[:, :, t0:t0 + T].rearrange("c p t -> p c t"), sx, reads=[B_xT[ti]], writes=[B_x])
            h, B_h, _ = hr.next()
            norm_mod(xt, B_x, h, B_h, sqr, rs, B_rs, tmpr, T, l, 1, wh, 6)
            return h, B_h
        return prep

    def phase_inproj1(l):
        with ExitStack() as st:
            W = st.enter_context(nc.sbuf_tensor(U("Wz"), [128, KC, 4096], BF16)); B_W = Buf()
            sW = sem_cached("wA")
            wv = w_in[l].rearrange("(k p) n -> p k n", p=128)
            load_w(W[:, :, 0:2048], B_W, wv[:, :, C_Z:C_Z + 2048], sW)
            load_w(W[:, :, 2048:4096], B_W, wv[:, :, C_GS:C_GS + 2048], sW)
            prep = inproj_common(st)
            zr = Ring(p, st, "zo", 2, [128, 2048], F32)
            gr = Ring(p, st, "go", 2, [128, 4, 512], F32)
            for ti in range(NT):
                t0, T, wh = tiles[ti]
                h, B_h = prep(ti, l)
                for sub in range(T // 128):
                    zt, B_zt, sz = zr.next()
                    for cb in range(4):
                        pb = cb % 4
                        for kc in range(KC):
                            p.pe(C("matmul",
                                PS[pb][:, :], lhsT=h[:, kc, sub * 128:(sub + 1) * 128], rhs=W[:, kc, cb * 512:(cb + 1) * 512],
                                start=(kc == 0), stop=(kc == KC - 1)), reads=[B_W, B_h], writes=[B_PS[pb]])
                        p.act(C("activation", out=zt[:, cb * 512:(cb + 1) * 512], in_=PS[pb][:, :], func=AF.Silu),
                              reads=[B_PS[pb]], writes=[B_zt])
                    p.dma("pool", z_tok[t0 + sub * 128:t0 + (sub + 1) * 128, :], zt[:], sz, reads=[B_zt], writes=[B_z])
                for gq in range(4):
                    gt_, B_gt, sg = gr.next()
                    for c4 in range(4):
                        ch = gq * 4 + c4
                        pb = 4 + (c4 % 2)
                        for kc in range(KC):
                            p.pe(C("matmul",
                                PS[pb][:, :T], lhsT=W[:, kc, 2048 + ch * 128:2048 + (ch + 1) * 128], rhs=h[:, kc, :T],
                                start=(kc == 0), stop=(kc == KC - 1)), reads=[B_W, B_h], writes=[B_PS[pb]])
                        p.act(C("activation", out=gt_[:, c4, :T], in_=PS[pb][:, :T], func=AF.Sigmoid),
                              reads=[B_PS[pb]], writes=[B_gt])
                    p.dma("pool", gT[gq * 4:(gq + 1) * 4, :, t0:t0 + T].rearrange("c p t -> p c t"), gt_[:, :, :T], sg,
                          reads=[B_gt], writes=[B_g])
            p.barrier()

    def phase_inproj2(l):
        with ExitStack() as st:
            W = st.enter_context(nc.sbuf_tensor(U("Wx"), [128, KC, 3136], BF16)); B_W = Buf()
            sW = sem_cached("wA")
            wv = w_in[l].rearrange("(k p) n -> p k n", p=128)
            load_w(W[:, :, :], B_W, wv[:, :, C_XBC:C_XBC + 3136], sW)
            prep = inproj_common(st)
            orr = Ring(p, st, "xo", 2, [128, 4, 512], F32)
            dr = Ring(p, st, "do", 2, [128, 4, 64], F32)
            for ti in range(NT):
                t0, T, wh = tiles[ti]
                h, B_h = prep(ti, l)
                for gq in range(6):
                    ot, B_ot, so = orr.next()
                    for c4 in range(4):
                        ch = gq * 4 + c4
                        pb = c4 % 4
                        for kc in range(KC):
                            p.pe(C("matmul",
                                PS[pb][:, :T], lhsT=W[:, kc, ch * 128:(ch + 1) * 128], rhs=h[:, kc, :T],
                                start=(kc == 0), stop=(kc == KC - 1)), reads=[B_W, B_h], writes=[B_PS[pb]])
                        if c4 % 2 == 0:
                            p.act(C("activation", out=ot[:, c4, :T], in_=PS[pb][:, :T], func=AF.Copy),
                                  reads=[B_PS[pb]], writes=[B_ot])
                        else:
                            p.dve(C("tensor_copy", out=ot[:, c4, :T], in_=PS[pb][:, :T]),
                                  reads=[B_PS[pb]], writes=[B_ot])
                    p.dma("pool", xbcp[gq * 4:(gq + 1) * 4, :, t0:t0 + T].rearrange("c p t -> p c t"), ot[:, :, :T], so,
                          reads=[B_ot], writes=[B_xbcp])
                dt_, B_dt, sd = dr.next()
                for sub in range(T // 128):
                    for kc in range(KC):
                        p.pe(C("matmul",
                            PS[4][:, sub * 64:(sub + 1) * 64], lhsT=h[:, kc, sub * 128:(sub + 1) * 128], rhs=W[:, kc, 3072:3136],
                            start=(kc == 0), stop=(kc == KC - 1)), reads=[B_W, B_h], writes=[B_PS[4]])
                nsub = T // 128
                p.dve(C("tensor_copy",
                    out=dt_[:, :nsub, :], in_=PS[4][:, :nsub * 64].rearrange("p (a b) -> p a b", b=64)),
                    reads=[B_PS[4]], writes=[B_dt])
                p.dma("pool", dtraw[t0:t0 + T, :].rearrange("(a p) j -> p a j", p=128), dt_[:, :nsub, :], sd,
                      reads=[B_dt], writes=[B_dtraw])
            p.barrier()

    def phase_inproj3(l):
        with ExitStack() as st:
            W = st.enter_context(nc.sbuf_tensor(U("Wq"), [128, KC, 2816], BF16)); B_W = Buf()
            sW = sem_cached("wA")
            wv = w_in[l].rearrange("(k p) n -> p k n", p=128)
            load_w(W[:, :, 0:1280], B_W, wv[:, :, C_Q:C_Q + 1280], sW)
            load_w(W[:, :, 2560:2816], B_W, wv[:, :, C_V:C_V + 256], sW)
            for kc in range(KC):
                src = wv[:, kc, C_Q:C_Q + 1280].rearrange("p (a two f) -> p a two f", two=2, f=32)
                dst = W[:, kc, 1280:2560].rearrange("p (a two f) -> p a two f", two=2, f=32)
                load_w(dst[:, :, 0, :], B_W, src[:, :, 1, :], sW)
                load_w(dst[:, :, 1, :], B_W, src[:, :, 0, :], sW)
            prep = inproj_common(st)
            G = st.enter_context(nc.sbuf_tensor(U("qkg_s"), [128, 4], F32)); B_G = Buf()
            p.dma("sp", G[:], qkg[:, l, :], s_misc, reads=[B_const], writes=[B_G])
            cr = Ring(p, st, "cs_", 2, [128, 2, 512], F32)
            qo = Ring(p, st, "qo", 2, [128, 10, 512], BF16)
            vo = Ring(p, st, "vo", 2, [128, 4, 256], BF16)
            sq2 = st.enter_context(nc.sbuf_tensor(U("sq2"), [128, 512], BF16)); B_sq2 = Buf()
            rs2 = st.enter_context(nc.sbuf_tensor(U("rs2"), [128, 512], F32)); B_rs2 = Buf()
            t1 = st.enter_context(nc.sbuf_tensor(U("t1"), [128, 512], F32)); B_t1 = Buf()
            t2 = st.enter_context(nc.sbuf_tensor(U("t2"), [128, 512], F32)); B_t2 = Buf()
            for ti in range(NT):
                t0, T, wh = tiles[ti]
                h, B_h = prep(ti, l)
                ct, B_ct, sc_ = cr.next()
                p.dma("sp", ct[:, 0, :T], cosT[:, t0:t0 + T], sc_, reads=[B_const], writes=[B_ct])
                p.dma("sp", ct[:, 1, :T], sinT[:, t0:t0 + T], sc_, reads=[B_const], writes=[B_ct])
                qt, B_qt, sq_ = qo.next()
                for hd in range(10):
                    gi = 0 if hd < 8 else 1
                    pa, pr = (0, 1) if hd % 2 == 0 else (2, 3)
                    for kc in range(KC):
                        p.pe(C("matmul",
                            PS[pa][:, :T], lhsT=W[:, kc, hd * 128:(hd + 1) * 128], rhs=h[:, kc, :T],
                            start=(kc == 0), stop=(kc == KC - 1)), reads=[B_W, B_h], writes=[B_PS[pa]])
                    for kc in range(KC):
                        p.pe(C("matmul",
                            PS[pr][:, :T], lhsT=W[:, kc, 1280 + hd * 128:1280 + (hd + 1) * 128], rhs=h[:, kc, :T],
                            start=(kc == 0), stop=(kc == KC - 1)), reads=[B_W, B_h], writes=[B_PS[pr]])
                    p.act(C("activation", out=sq2[:, :T], in_=PS[pa][:, :T], func=AF.Square),
                          reads=[B_PS[pa]], writes=[B_sq2])
                    p.pe(C("matmul", PS[5][:, :T], lhsT=ones_b[:], rhs=sq2[:, :T], start=True, stop=True),
                         reads=[B_sq2, B_glob], writes=[B_PS[5]])
                    p.act(C("activation", out=rs2[:, :T], in_=PS[5][:, :T], func=AF.Sqrt, bias=EPS, scale=1.0 / 128),
                          reads=[B_PS[5]], writes=[B_rs2])
                    p.dve(C("reciprocal", out=rs2[:, :T], in_=rs2[:, :T]), reads=[B_rs2], writes=[B_rs2])
                    p.dve(C("scalar_tensor_tensor",
                        out=t1[:, :T], in0=PS[pa][:, :T], scalar=G[:, gi:gi + 1], in1=ct[:, 0, :T], op0=ALU.mult, op1=ALU.mult),
                        reads=[B_PS[pa], B_G, B_ct], writes=[B_t1])
                    p.dve(C("scalar_tensor_tensor",
                        out=t2[:, :T], in0=PS[pr][:, :T], scalar=G[:, 2 + gi:3 + gi], in1=ct[:, 1, :T], op0=ALU.mult, op1=ALU.mult),
                        reads=[B_PS[pr], B_G, B_ct], writes=[B_t2])
                    p.dve(C("tensor_tensor", out=t1[:, :T], in0=t1[:, :T], in1=t2[:, :T], op=ALU.add),
                          reads=[B_t1, B_t2], writes=[B_t1])
                    p.dve(C("tensor_tensor", out=qt[:, hd, :T], in0=t1[:, :T], in1=rs2[:, :T], op=ALU.mult),
                          reads=[B_t1, B_rs2], writes=[B_qt])
                p.dma("pool", qT[:, :, t0:t0 + T].rearrange("c p t -> p c t"), qt[:, 0:8, :T], sq_, reads=[B_qt], writes=[B_q])
                p.dma("pool", kT[:, :, t0:t0 + T].rearrange("c p t -> p c t"), qt[:, 8:10, :T], sq_, reads=[B_qt], writes=[B_k])
                vt, B_vt, sv = vo.next()
                nsub = T // 128
                for sub in range(nsub):
                    for kc in range(KC):
                        p.pe(C("matmul",
                            PS[4][:, (sub % 2) * 256:(sub % 2 + 1) * 256], lhsT=h[:, kc, sub * 128:(sub + 1) * 128], rhs=W[:, kc, 2560:2816],
                            start=(kc == 0), stop=(kc == KC - 1)), reads=[B_W, B_h], writes=[B_PS[4]])
                    p.act(C("activation", out=vt[:, sub, :], in_=PS[4][:, (sub % 2) * 256:(sub % 2 + 1) * 256], func=AF.Copy),
                          reads=[B_PS[4]], writes=[B_vt])
                p.dma("pool", v_tok[t0:t0 + T, :].rearrange("(a p) j -> p a j", p=128), vt[:, :nsub, :], sv,
                      reads=[B_vt], writes=[B_v])
            p.barrier()

    def phase_conv(l):
        with ExitStack() as st:
            cw = st.enter_context(nc.sbuf_tensor(U("cw"), [128, 3, 24], F32)); B_cw = Buf()
            cb = st.enter_context(nc.sbuf_tensor(U("cb"), [128, 24], F32))
            p.dma("sp", cw[:], convw[:, l, :, :], s_misc, reads=[B_const], writes=[B_cw])
            p.dma("sp", cb[:], convb[:, l, :], s_misc, reads=[B_const], writes=[B_cw])
            ur = Ring(p, st, "cu", 3, [128, 514], F32)
            acc = [st.enter_context(nc.sbuf_tensor(U("acc%d" % i), [128, 512], F32)) for i in range(2)]
            B_acc = [Buf(), Buf()]
            orr = Ring(p, st, "co", 2, [128, 4, 512], BF16)
            tk = Ring(p, st, "tk", 2, [128, 4, 2560], BF16)
            for ti in range(NT):
                t0, T, wh = tiles[ti]
                left = (ti <= 1)
                right = (ti == 0 or ti == NT - 1)
                tkt, B_tk, stk = tk.next()
                nsub = T // 128
                for gq in range(6):
                    ot, B_ot, so = orr.next()
                    for c4 in range(4):
                        ch = gq * 4 + c4
                        ut, B_u, su = ur.next()
                        lo = 0 if not left else 1
                        hi = T + 2 if not right else T + 1
                        if left:
                            p.dve(C("memset", ut[:, 0:1], 0.0), writes=[B_u])
                        if right:
                            p.dve(C("memset", ut[:, T + 1:T + 2], 0.0), writes=[B_u])
                        p.dma("sp", ut[:, lo:hi], xbcp[ch, :, t0 - 1 + lo:t0 - 1 + hi], su, reads=[B_xbcp], writes=[B_u])
                        a = acc[c4 % 2]; B_a = B_acc[c4 % 2]
                        p.act(C("activation", out=a[:, :T], in_=ut[:, 1:T + 1], func=AF.Identity,
                                                                       bias=cb[:, ch:ch + 1], scale=cw[:, 1, ch:ch + 1]),
                              reads=[B_u, B_cw], writes=[B_a])
                        p.dve(C("scalar_tensor_tensor",
                            out=a[:, :T], in0=ut[:, 0:T], scalar=cw[:, 0, ch:ch + 1], in1=a[:, :T], op0=ALU.mult, op1=ALU.add),
                            reads=[B_u, B_cw, B_a], writes=[B_a])
                        p.dve(C("scalar_tensor_tensor",
                            out=a[:, :T], in0=ut[:, 2:T + 2], scalar=cw[:, 2, ch:ch + 1], in1=a[:, :T], op0=ALU.mult, op1=ALU.add),
                            reads=[B_u, B_cw, B_a], writes=[B_a])
                        p.act(C("activation", out=ot[:, c4, :T], in_=a[:, :T], func=AF.Silu),
                              reads=[B_a], writes=[B_ot])
                        if ch < 20:
                            for sub in range(nsub):
                                p.pe(C("transpose",
                                    out=PSB[:, sub * 128:(sub + 1) * 128], in_=ot[:, c4, sub * 128:(sub + 1) * 128], identity=ident_b[:]),
                                    reads=[B_ot, B_glob], writes=[B_PSB])
                            p.dve(C("tensor_copy",
                                out=tkt[:, :nsub, ch * 128:(ch + 1) * 128],
                                in_=PSB[:, :nsub * 128].rearrange("p (a b) -> p a b", b=128)),
                                reads=[B_PSB], writes=[B_tk])
                    if gq < 4:
                        p.dma("pool", xsT[gq * 4:(gq + 1) * 4, :, t0:t0 + T].rearrange("c p t -> p c t"), ot[:, :, :T], so,
                              reads=[B_ot], writes=[B_xsT])
                    else:
                        p.dma("pool", bcT[(gq - 4) * 4:(gq - 3) * 4, :, t0:t0 + T].rearrange("c p t -> p c t"), ot[:, :, :T], so,
                              reads=[B_ot], writes=[B_bcT])
                p.dma("pool", xb_tok[t0:t0 + T, :].rearrange("(a p) j -> p a j", p=128), tkt[:, :nsub, :], stk,
                      reads=[B_tk], writes=[B_xbtok])
            p.barrier()

    def phase_ssd(l, last):
        NQ = NCHK * 64
        with ExitStack() as st0:
            dt = st0.enter_context(nc.sbuf_tensor(U("dt"), [128, NCHK, 64], F32)); B_dt = Buf()
            wst = st0.enter_context(nc.sbuf_tensor(U("wst"), [128, NCHK, 64], F32)); B_wst = Buf()
            eoff = st0.enter_context(nc.sbuf_tensor(U("eoff"), [128, NCHK, 64], F32)); B_eoff = Buf()
            cdec = st0.enter_context(nc.sbuf_tensor(U("cdec"), [128, NCHK, 64], F32)); B_cdec = Buf()
            stA = ExitStack()
            dA = stA.enter_context(nc.sbuf_tensor(U("dA"), [128, NCHK, 64], F32)); B_dA = Buf()
            cs = stA.enter_context(nc.sbuf_tensor(U("cs"), [128, NCHK, 64], F32)); B_cs = Buf()
            tot = stA.enter_context(nc.sbuf_tensor(U("tot"), [128, NCHK, 64], F32)); B_tot = Buf()
            with ExitStack() as st:
                al = st.enter_context(nc.sbuf_tensor(U("al"), [128, 64], F32)); B_al = Buf()
                db = st.enter_context(nc.sbuf_tensor(U("db"), [128, 64], F32)); B_db = Buf()
                tmpa = st.enter_context(nc.sbuf_tensor(U("tmpa"), [128, NCHK, 64], F32)); B_tmpa = Buf()
                p.dma("sp", al[:], alog[:, l, :], s_misc, reads=[B_const], writes=[B_al])
                p.dma("sp", db[:], dtb[:, l, :], s_misc, reads=[B_const], writes=[B_db])
                p.dma("sp", dt[:], dtraw.rearrange("(c p) j -> p c j", p=128), s_misc, reads=[B_dtraw], writes=[B_dt])
                p.act(C("activation", out=al[:], in_=al[:], func=AF.Exp), reads=[B_al], writes=[B_al])
                p.dve(C("tensor_tensor", out=dt[:], in0=dt[:], in1=db[:].unsqueeze(1).to_broadcast([128, NCHK, 64]), op=ALU.add),
                      reads=[B_dt, B_db], writes=[B_dt])
                p.act(C("activation", out=dt[:], in_=dt[:], func=AF.Exp), reads=[B_dt], writes=[B_dt])
                p.act(C("activation", out=dt[:], in_=dt[:], func=AF.Ln, bias=1.0, scale=1.0), reads=[B_dt], writes=[B_dt])
                p.dve(C("scalar_tensor_tensor", out=dA[:], in0=dt[:], scalar=-1.0,
                                                       in1=al[:].unsqueeze(1).to_broadcast([128, NCHK, 64]), op0=ALU.mult, op1=ALU.mult),
                      reads=[B_dt, B_al], writes=[B_dA])
                dAf = dA[:].rearrange("p c j -> p (c j)")
                csf = cs[:].rearrange("p c j -> p (c j)")
                totf = tot[:].rearrange("p c j -> p (c j)")
                for i0 in range(0, NQ, 512):
                    n = min(512, NQ - i0)
                    p.pe(C("matmul", PS[0][:, :n], lhsT=masks[:, 0, :], rhs=dAf[:, i0:i0 + n], start=True, stop=True),
                         reads=[B_dA, B_glob], writes=[B_PS[0]])
                    p.dve(C("tensor_copy", out=csf[:, i0:i0 + n], in_=PS[0][:, :n]), reads=[B_PS[0]], writes=[B_cs])
                    p.pe(C("matmul", PS[1][:, :n], lhsT=ones_f[:], rhs=dAf[:, i0:i0 + n], start=True, stop=True),
                         reads=[B_dA, B_glob], writes=[B_PS[1]])
                    p.dve(C("tensor_copy", out=totf[:, i0:i0 + n], in_=PS[1][:, :n]), reads=[B_PS[1]], writes=[B_tot])
                p.act(C("activation", out=cdec[:], in_=tot[:], func=AF.Exp), reads=[B_tot], writes=[B_cdec])
                vT = st.enter_context(nc.sbuf_tensor(U("vT"), [64, NCHK, 128], F32)); B_vT = Buf()
                vr = st.enter_context(nc.sbuf_tensor(U("vr"), [64, NCHK, 128], F32)); B_vr = Buf()
                vhl = st.enter_context(nc.sbuf_tensor(U("vhl"), [64, NCHK, 3, 128], BF16)); B_vhl = Buf()
                for c0 in range(0, NCHK, 4):
                    nn = min(4, NCHK - c0)
                    for j in range(nn):
                        p.pe(C("matmul", PS[2][0:64, j * 128:(j + 1) * 128], lhsT=dA[:, c0 + j, :], rhs=masks[:, 0, :], start=True, stop=True),
                             reads=[B_dA, B_glob], writes=[B_PS[2]])
                        p.pe(C("matmul", PS[3][0:64, j * 128:(j + 1) * 128], lhsT=dA[:, c0 + j, :], rhs=masks[:, 3, :], start=True, stop=True),
                             reads=[B_dA, B_glob], writes=[B_PS[3]])
                    p.act(C("activation", out=vT[0:32, c0:c0 + nn, :], in_=PS[2][0:32, 0:nn * 128].rearrange("p (a b) -> p a b", b=128), func=AF.Copy),
                          reads=[B_PS[2]], writes=[B_vT])
                    p.act(C("activation", out=vT[32:64, c0:c0 + nn, :], in_=PS[3][32:64, 0:nn * 128].rearrange("p (a b) -> p a b", b=128), func=AF.Copy, scale=-1.0),
                          reads=[B_PS[3]], writes=[B_vT])
                p.dve(C("tensor_copy", out=vhl[:, :, 0, :], in_=vT[:]), reads=[B_vT], writes=[B_vhl])
                p.dve(C("tensor_tensor", out=vr[:], in0=vT[:], in1=vhl[:, :, 0, :], op=ALU.subtract), reads=[B_vT, B_vhl], writes=[B_vr])
                p.dve(C("tensor_copy", out=vhl[:, :, 1, :], in_=vr[:]), reads=[B_vr], writes=[B_vhl])
                p.dve(C("tensor_tensor", out=vhl[:, :, 2, :], in0=vr[:], in1=vhl[:, :, 1, :], op=ALU.subtract), reads=[B_vr, B_vhl], writes=[B_vhl])
                p.dma("pool", csD.rearrange("c r h l -> r c h l"), vhl[:], s_misc, reads=[B_vhl], writes=[B_csD])
                p.dve(C("tensor_tensor", out=tmpa[:, :, 0:32], in0=tot[:, :, 0:32], in1=cs[:, :, 0:32], op=ALU.subtract),
                      reads=[B_tot, B_cs], writes=[B_tmpa])
                p.dve(C("tensor_tensor", out=tmpa[:, :, 32:64], in0=cs[:, :, 32:64], in1=dA[:, :, 32:64], op=ALU.subtract),
                      reads=[B_cs, B_dA], writes=[B_tmpa])
                p.act(C("activation", out=wst[:], in_=tmpa[:], func=AF.Exp), reads=[B_tmpa], writes=[B_wst])
                p.dve(C("tensor_tensor", out=wst[:], in0=wst[:], in1=dt[:], op=ALU.mult), reads=[B_wst, B_dt], writes=[B_wst])
                p.act(C("activation", out=eoff[:, :, 0:32], in_=cs[:, :, 0:32], func=AF.Exp), reads=[B_cs], writes=[B_eoff])
                p.dve(C("tensor_tensor", out=tmpa[:, :, 32:64], in0=tot[:, :, 32:64], in1=tmpa[:, :, 32:64], op=ALU.subtract),
                      reads=[B_tot, B_tmpa, B_wst], writes=[B_tmpa])
                p.act(C("activation", out=eoff[:, :, 32:64], in_=tmpa[:, :, 32:64], func=AF.Exp), reads=[B_tmpa], writes=[B_eoff])
                p.barrier()
            stA.close()
            with ExitStack() as st:
                Sst = [st.enter_context(nc.sbuf_tensor(U("Sst%d" % d), [128, 4, 512], F32)) for d in range(2)]
                B_S = [Buf(), Buf()]
                xr = Ring(p, st, "sx", 2, [128, 2560], BF16)
                xw = Ring(p, st, "sxw", 3, [128, 512], BF16)
                so_ = Ring(p, st, "sso", 3, [128, 4, 512], BF16)
                stmp = st.enter_context(nc.sbuf_tensor(U("stmp"), [128, 512], F32)); B_stmp = Buf()
                for d in range(2):
                    order = list(range(NCHK)) if d == 0 else [1, 0] + list(range(NCHK - 1, 1, -1))
                    p.dve(C("memset", Sst[d][:], 0.0), writes=[B_S[d]])
                    for c in order:
                        sot, B_sot, sso = so_.next()
                        p.act(C("activation", out=sot[:], in_=Sst[d][:], func=AF.Copy), reads=[B_S[d]], writes=[B_sot])
                        p.dma("pool", sin_d[d][c].rearrange("g p n -> p g n"), sot[:], sso, reads=[B_sot], writes=[B_sin[d]])
                        if c == order[-1]:
                            break
                        xt, B_x, sx = xr.next()
                        p.dma("sp", xt[:], xb_tok[c * 128:(c + 1) * 128, :], sx, reads=[B_xbtok], writes=[B_x])
                        for g in range(4):
                            xwt, B_xw, _ = xw.next()
                            p.pool(C("tensor_tensor",
                                out=xwt[:].rearrange("p (h q) -> p h q", q=64), in0=xt[:, g * 512:(g + 1) * 512].rearrange("p (h q) -> p h q", q=64),
                                in1=wst[:, c, d * 32 + g * 8:d * 32 + g * 8 + 8].unsqueeze(2).to_broadcast([128, 8, 64]), op=ALU.mult),
                                reads=[B_x, B_wst], writes=[B_xw])
                            pb = g % 4
                            p.pe(C("matmul",
                                PS[pb][:, :], lhsT=xt[:, 2048 + g * 128:2048 + (g + 1) * 128], rhs=xwt[:], start=True, stop=True),
                                reads=[B_x, B_xw], writes=[B_PS[pb]])
                            p.dve(C("tensor_tensor",
                                out=stmp[:].rearrange("p (h q) -> p h q", q=64), in0=Sst[d][:, g, :].rearrange("p (h q) -> p h q", q=64),
                                in1=cdec[:, c, d * 32 + g * 8:d * 32 + g * 8 + 8].unsqueeze(2).to_broadcast([128, 8, 64]), op=ALU.mult),
                                reads=[B_S[d], B_cdec, B_sot], writes=[B_stmp])
                            p.dve(C("tensor_tensor", out=Sst[d][:, g, :], in0=stmp[:], in1=PS[pb][:, :], op=ALU.add),
                                  reads=[B_stmp, B_PS[pb]], writes=[B_S[d]])
                p.barrier()
            with ExitStack() as st:
                dsk = st.enter_context(nc.sbuf_tensor(U("dsk"), [128, 32], F32)); B_dsk = Buf()
                gn = st.enter_context(nc.sbuf_tensor(U("gn"), [128, DIN], F32))
                p.dma("sp", dsk[:], dskip[:, l, :], s_misc, reads=[B_const], writes=[B_dsk])
                p.dma("sp", gn[:], sng[:, l, :], s_misc, reads=[B_const], writes=[B_dsk])
                xr = Ring(p, st, "yx", 2, [128, 2560], BF16)
                bcr = Ring(p, st, "ybc", 2, [128, 8, 128], BF16)
                sir = Ring(p, st, "ysi", 2, [128, 2, 4, 512], BF16)
                zr = Ring(p, st, "yz", 2, [128, DIN], F32)
                Grep = st.enter_context(nc.sbuf_tensor(U("Grep"), [128, 4, 4, 128], BF16)); B_Gm = Buf()
                xd = st.enter_context(nc.sbuf_tensor(U("xd"), [128, 2, DIN], BF16)); B_xd = Buf()
                subr = Ring(p, st, "ysub", 2, [12, 16, 128], BF16)
                bcvr = Ring(p, st, "ybcv", 2, [3, 16, 512], BF16)
                Em = [st.enter_context(nc.sbuf_tensor(U("Em%d" % i), [128, 2, 512], BF16)) for i in range(2)]
                B_Em = [Buf(), Buf()]
                Mm = [st.enter_context(nc.sbuf_tensor(U("Mm%d" % i), [128, 2, 4, 128], BF16)) for i in range(2)]
                B_Mm = [Buf(), Buf()]
                ya = st.enter_context(nc.sbuf_tensor(U("ya"), [128, DIN], F32)); B_ya = Buf()
                yt = st.enter_context(nc.sbuf_tensor(U("ytm"), [128, 512], F32)); B_yt = Buf()
                ssq = st.enter_context(nc.sbuf_tensor(U("ssq"), [128, 8], F32)); B_ssq = Buf()
                junk = st.enter_context(nc.sbuf_tensor(U("junk"), [128, 512], F32)); B_junk = Buf()
                yb = st.enter_context(nc.sbuf_tensor(U("yb"), [128, DIN], BF16)); B_yb = Buf()
                yo = Ring(p, st, "yo", 2, [128, 16, 128], BF16)
                chunks = list(range(2, NCHK)) if last else list(range(NCHK))
                it = 0
                for c in chunks:
                    xt, B_x, sx = xr.next()
                    p.dma("sp", xt[:], xb_tok[c * 128:(c + 1) * 128, :], sx, reads=[B_xbtok], writes=[B_x])
                    bct, B_bc, sbc = bcr.next()
                    p.dma("sp", bct[:], bcT[:, :, c * 128:(c + 1) * 128].rearrange("c p t -> p c t"), sbc, reads=[B_bcT], writes=[B_bc])
                    sit, B_si, ssi = sir.next()
                    for d in range(2):
                        p.dma("sp", sit[:, d, :, :], sin_d[d][c].rearrange("g p n -> p g n"), ssi, reads=[B_sin[d]], writes=[B_si])
                    zt, B_zt, sz = zr.next()
                    p.dma("sp", zt[:], z_tok[c * 128:(c + 1) * 128, :], sz, reads=[B_z], writes=[B_zt])
                    subt, B_sub, ssub = subr.next()
                    p.dma("sp", subt[:], csD[c].rearrange("(g j) h l -> (j h) g l", j=4), ssub, reads=[B_csD], writes=[B_sub])
                    bcv, B_bcv, sbcv = bcvr.next()
                    p.dma("sp", bcv[:].rearrange("h g (j l) -> h g j l", l=128), csD[c].rearrange("(g j) h l -> h g j l", j=4), sbcv,
                          reads=[B_csD], writes=[B_bcv])
                    for g in range(4):
                        p.pe(C("matmul", PS[6][:, g * 128:(g + 1) * 128], lhsT=bct[:, g, :], rhs=bct[:, 4 + g, :], start=True, stop=True),
                             reads=[B_bc], writes=[B_PS[6]])
                    for g in range(4):
                        p.dve(C("tensor_copy", out=Grep[:, g, :, :],
                                in_=PS[6][:, g * 128:(g + 1) * 128].unsqueeze(1).to_broadcast([128, 4, 128])),
                              reads=[B_PS[6]], writes=[B_Gm])
                    for d in range(2):
                        p.pool(C("tensor_tensor",
                            out=xd[:, d, :].rearrange("p (h q) -> p h q", q=64), in0=xt[:, 0:2048].rearrange("p (h q) -> p h q", q=64),
                            in1=dt[:, c, d * 32:(d + 1) * 32].unsqueeze(2).to_broadcast([128, 32, 64]), op=ALU.mult),
                            reads=[B_x, B_dt], writes=[B_xd])
                    for g in range(4):
                        ydb = g % 2
                        for hq in range(2):
                            k = it % 2
                            it += 1
                            for d in range(2):
                                gi = d * 8 + g * 2 + hq
                                sb = 2 + d
                                p.pe(C("matmul", PS[sb][:, :], lhsT=ones_b[0:3, :], rhs=bcv[0:3, gi, :], start=True, stop=False),
                                     reads=[B_glob, B_bcv], writes=[B_PS[sb]])
                                p.pe(C("matmul", PS[sb][:, :], lhsT=subt[0:12, gi, :], rhs=negind[0:12, :], start=False, stop=False),
                                     reads=[B_glob, B_sub], writes=[B_PS[sb]])
                                p.pe(C("matmul", PS[sb][:, :], lhsT=ident_b[:], rhs=negmask[:, d, :], start=False, stop=True),
                                     reads=[B_glob], writes=[B_PS[sb]])
                                p.act(C("activation", out=Em[k][:, d, :], in_=PS[sb][:, :], func=AF.Exp),
                                      reads=[B_PS[sb]], writes=[B_Em[k]])
                                p.dve(C("tensor_tensor", out=Mm[k][:, d, :, :].rearrange("p j l -> p (j l)"), in0=Em[k][:, d, :],
                                        in1=Grep[:, g, :, :].rearrange("p j l -> p (j l)"), op=ALU.mult),
                                      reads=[B_Em[k], B_Gm], writes=[B_Mm[k]])
                            for j in range(4):
                                hh = g * 8 + hq * 4 + j
                                col = (hq * 4 + j) * 64
                                for d in range(2):
                                    p.pe(C("matmul",
                                        PS[ydb][:, col:col + 64], lhsT=Mm[k][:, d, j, :], rhs=xd[:, d, hh * 64:(hh + 1) * 64],
                                        start=(d == 0), stop=(d == 1)), reads=[B_Mm[k], B_xd], writes=[B_PS[ydb]])
                        for d in range(2):
                            ob = 4 + d
                            p.pe(C("matmul",
                                PS[ob][:, :], lhsT=bct[:, 4 + g, :], rhs=sit[:, d, g, :], start=True, stop=True),
                                reads=[B_bc, B_si], writes=[B_PS[ob]])
                        p.dve(C("tensor_tensor",
                            out=yt[:].rearrange("p (h q) -> p h q", q=64), in0=PS[4][:, :].rearrange("p (h q) -> p h q", q=64),
                            in1=eoff[:, c, g * 8:g * 8 + 8].unsqueeze(2).to_broadcast([128, 8, 64]), op=ALU.mult),
                            reads=[B_PS[4], B_eoff], writes=[B_yt])
                        p.dve(C("tensor_tensor", out=ya[:, g * 512:(g + 1) * 512], in0=yt[:], in1=PS[ydb][:, :], op=ALU.add),
                              reads=[B_yt, B_PS[ydb]], writes=[B_ya])
                        p.dve(C("tensor_tensor",
                            out=yt[:].rearrange("p (h q) -> p h q", q=64), in0=PS[5][:, :].rearrange("p (h q) -> p h q", q=64),
                            in1=eoff[:, c, 32 + g * 8:32 + g * 8 + 8].unsqueeze(2).to_broadcast([128, 8, 64]), op=ALU.mult),
                            reads=[B_PS[5], B_eoff, B_ya], writes=[B_yt])
                        p.dve(C("tensor_tensor", out=ya[:, g * 512:(g + 1) * 512], in0=ya[:, g * 512:(g + 1) * 512], in1=yt[:], op=ALU.add),
                              reads=[B_yt, B_ya], writes=[B_ya])
                        p.dve(C("tensor_tensor",
                            out=yt[:].rearrange("p (h q) -> p h q", q=64), in0=xt[:, g * 512:(g + 1) * 512].rearrange("p (h q) -> p h q", q=64),
                            in1=dsk[:, g * 8:g * 8 + 8].unsqueeze(2).to_broadcast([128, 8, 64]), op=ALU.mult),
                            reads=[B_x, B_dsk, B_ya], writes=[B_yt])
                        p.dve(C("tensor_tensor", out=ya[:, g * 512:(g + 1) * 512], in0=ya[:, g * 512:(g + 1) * 512], in1=yt[:], op=ALU.add),
                              reads=[B_yt, B_ya], writes=[B_ya])
                        p.dve(C("tensor_tensor", out=ya[:, g * 512:(g + 1) * 512], in0=ya[:, g * 512:(g + 1) * 512],
                                                                   in1=zt[:, g * 512:(g + 1) * 512], op=ALU.mult),
                              reads=[B_zt, B_ya], writes=[B_ya])
                        p.act(C("activation", out=junk[:], in_=ya[:, g * 512:(g + 1) * 512], func=AF.Square, accum_out=ssq[:, g:g + 1]),
                              reads=[B_ya], writes=[B_junk, B_ssq])
                        p.act(C("activation", out=ssq[:, 4 + g:5 + g], in_=ssq[:, g:g + 1], func=AF.Sqrt, bias=EPS, scale=1.0 / 512),
                              reads=[B_ssq], writes=[B_ssq])
                        p.dve(C("reciprocal", out=ssq[:, 4 + g:5 + g], in_=ssq[:, 4 + g:5 + g]), reads=[B_ssq], writes=[B_ssq])
                        p.dve(C("scalar_tensor_tensor",
                            out=yb[:, g * 512:(g + 1) * 512], in0=ya[:, g * 512:(g + 1) * 512], scalar=ssq[:, 4 + g:5 + g],
                            in1=gn[:, g * 512:(g + 1) * 512], op0=ALU.mult, op1=ALU.mult),
                            reads=[B_ya, B_ssq, B_dsk], writes=[B_yb])
                    yot, B_yo, syo = yo.next()
                    for q8 in range(2):
                        for j in range(8):
                            cc = q8 * 8 + j
                            p.pe(C("transpose", out=PSB[:, j * 128:(j + 1) * 128], in_=yb[:, cc * 128:(cc + 1) * 128], identity=ident_b[:]),
                                 reads=[B_yb, B_glob], writes=[B_PSB])
                        p.act(C("activation", out=yot[:, q8 * 8:(q8 + 1) * 8, :], in_=PSB[:, :].rearrange("p (a b) -> p a b", b=128), func=AF.Copy),
                              reads=[B_PSB], writes=[B_yo])
                    p.dma("pool", yT[:, :, c * 128:(c + 1) * 128].rearrange("c p t -> p c t"), yot[:], syo, reads=[B_yo], writes=[B_yT])
                p.barrier()

    def phase_attn(l, last):
        with ExitStack() as st:
            Kt = st.enter_context(nc.sbuf_tensor(U("Kt"), [128, 2, TT], BF16)); B_K = Buf()
            Vt = st.enter_context(nc.sbuf_tensor(U("Vt"), [128, NCHK, 256], BF16)); B_V = Buf()
            p.dma("sp", Kt[:], kT.rearrange("c p t -> p c t"), s_misc, reads=[B_k], writes=[B_K])
            p.dma("sp", Vt[:], v_tok.rearrange("(a p) j -> p a j", p=128), s_misc, reads=[B_v], writes=[B_V])
            qr = Ring(p, st, "aq", 2, [128, 8, 512], BF16)
            pr = Ring(p, st, "ap", 4, [128, 512], BF16)
            ao = Ring(p, st, "ao", 2, [128, 8, 512], BF16)
            rd = st.enter_context(nc.sbuf_tensor(U("rd"), [128, 512], F32)); B_rd = Buf()
            sel = list(range(1, NT)) if last else list(range(NT))
            sidx = 0
            for ti in sel:
                t0, T, wh = tiles[ti]
                kts = [0, 1] if wh == 1 else list(range(NCHK))
                nk = len(kts)
                qt, B_qt, sq_ = qr.next()
                p.dma("sp", qt[:, :, :T], qT[:, :, t0:t0 + T].rearrange("c p t -> p c t"), sq_, reads=[B_q], writes=[B_qt])
                aot, B_ao, sao = ao.next()
                steps = [(hd, i, kt) for hd in range(8) for i, kt in enumerate(kts)]
                pend = None

                def do_pv(st_):
                    hd, i, kt, pt, B_pt = st_
                    kv = hd // 4
                    ob = 3 + (hd % 2)
                    db_ = 5 + (hd % 2)
                    p.pe(C("matmul", PS[ob][:, :T], lhsT=Vt[:, kt, kv * 128:(kv + 1) * 128], rhs=pt[:, :T],
                           start=(i == 0), stop=(i == nk - 1)), reads=[B_V, B_pt], writes=[B_PS[ob]])
                    p.pe(C("matmul", PS[db_][:, :T], lhsT=ones_b[:], rhs=pt[:, :T],
                           start=(i == 0), stop=(i == nk - 1)), reads=[B_glob, B_pt], writes=[B_PS[db_]])
                    if i == nk - 1:
                        p.dve(C("reciprocal", out=rd[:, :T], in_=PS[db_][:, :T]), reads=[B_PS[db_]], writes=[B_rd])
                        p.dve(C("tensor_tensor", out=aot[:, hd, :T], in0=PS[ob][:, :T], in1=rd[:, :T], op=ALU.mult),
                              reads=[B_PS[ob], B_rd], writes=[B_ao])

                for (hd, i, kt) in steps:
                    kv = hd // 4
                    sb = sidx % 3
                    sidx += 1
                    p.pe(C("matmul", PS[sb][:, :T], lhsT=Kt[:, kv, kt * 128:(kt + 1) * 128], rhs=qt[:, hd, :T], start=True, stop=True),
                         reads=[B_K, B_qt], writes=[B_PS[sb]])
                    pt, B_pt, _ = pr.next()
                    p.act(C("activation", out=pt[:, :T], in_=PS[sb][:, :T], func=AF.Exp, scale=ATT_SCALE),
                          reads=[B_PS[sb]], writes=[B_pt])
                    if pend is not None:
                        do_pv(pend)
                    pend = (hd, i, kt, pt, B_pt)
                do_pv(pend)
                p.dma("pool", attT[:, :, t0:t0 + T].rearrange("c p t -> p c t"), aot[:, :, :T], sao, reads=[B_ao], writes=[B_att])
            p.barrier()

    def phase_merge(l, last):
        with ExitStack() as st:
            Wso = st.enter_context(nc.sbuf_tensor(U("Wso"), [128, 16, D], BF16)); B_W = Buf()
            Wao = st.enter_context(nc.sbuf_tensor(U("Wao"), [128, KC, D], BF16))
            Wo = st.enter_context(nc.sbuf_tensor(U("Wo"), [128, KC, D], BF16))
            sW = sem_cached("wA")
            load_w(Wso[:], B_W, w_so[l].rearrange("(k p) n -> p k n", p=128), sW)
            load_w(Wao[:], B_W, w_ao[l].rearrange("(k p) n -> p k n", p=128), sW)
            load_w(Wo[:], B_W, w_o[l].rearrange("(k p) n -> p k n", p=128), sW)
            xr = Ring(p, st, "mx", 2, [128, KC, 512], F32)
            yr = Ring(p, st, "my", 2, [128, 16, 512], BF16)
            ar = Ring(p, st, "ma", 2, [128, KC, 512], BF16)
            gr = Ring(p, st, "mg", 3, [128, 2, 512], F32)
            m = st.enter_context(nc.sbuf_tensor(U("mm"), [128, KC, 512], BF16)); B_m = Buf()
            t1 = st.enter_context(nc.sbuf_tensor(U("mt1"), [128, 512], F32)); B_t1 = Buf()
            t2 = st.enter_context(nc.sbuf_tensor(U("mt2"), [128, 512], F32)); B_t2 = Buf()
            sel = list(range(1, NT)) if last else list(range(NT))
            for ti in sel:
                t0, T, wh = tiles[ti]
                xt, B_x, sx = xr.next()
                p.dma("sp", xt[:, :, :T], xT[:, :, t0:t0 + T].rearrange("c p t -> p c t"), sx, reads=[B_xT[ti]], writes=[B_x])
                yt, B_y, sy = yr.next()
                p.dma("sp", yt[:, :, :T], yT[:, :, t0:t0 + T].rearrange("c p t -> p c t"), sy, reads=[B_yT], writes=[B_y])
                at, B_a, sa_ = ar.next()
                p.dma("sp", at[:, :, :T], attT[:, :, t0:t0 + T].rearrange("c p t -> p c t"), sa_, reads=[B_att], writes=[B_a])
                for n in range(KC):
                    gt_, B_gt, sg = gr.next()
                    p.dma("sp", gt_[:, 0, :T], gT[n, :, t0:t0 + T], sg, reads=[B_g], writes=[B_gt])
                    p.dma("sp", gt_[:, 1, :T], gT[8 + n, :, t0:t0 + T], sg, reads=[B_g], writes=[B_gt])
                    pa, pb = (0, 1) if n % 2 == 0 else (2, 3)
                    for kc in range(16):
                        p.pe(C("matmul",
                            PS[pa][:, :T], lhsT=Wso[:, kc, n * 128:(n + 1) * 128], rhs=yt[:, kc, :T], start=(kc == 0), stop=(kc == 15)),
                            reads=[B_W, B_y], writes=[B_PS[pa]])
                    for kc in range(KC):
                        p.pe(C("matmul",
                            PS[pb][:, :T], lhsT=Wao[:, kc, n * 128:(n + 1) * 128], rhs=at[:, kc, :T], start=(kc == 0), stop=(kc == KC - 1)),
                            reads=[B_W, B_a], writes=[B_PS[pb]])
                    p.dve(C("tensor_tensor", out=t1[:, :T], in0=PS[pa][:, :T], in1=gt_[:, 0, :T], op=ALU.mult),
                          reads=[B_PS[pa], B_gt], writes=[B_t1])
                    p.dve(C("tensor_tensor", out=t2[:, :T], in0=PS[pb][:, :T], in1=gt_[:, 1, :T], op=ALU.mult),
                          reads=[B_PS[pb], B_gt], writes=[B_t2])
                    p.dve(C("tensor_tensor", out=m[:, n, :T], in0=t1[:, :T], in1=t2[:, :T], op=ALU.add),
                          reads=[B_t1, B_t2], writes=[B_m])
                for n in range(KC):
                    pb = 4 + n % 2
                    for kc in range(KC):
                        p.pe(C("matmul",
                            PS[pb][:, :T], lhsT=Wo[:, kc, n * 128:(n + 1) * 128], rhs=m[:, kc, :T], start=(kc == 0), stop=(kc == KC - 1)),
                            reads=[B_W, B_m], writes=[B_PS[pb]])
                    p.dve(C("scalar_tensor_tensor",
                        out=xt[:, n, :T], in0=PS[pb][:, :T], scalar=HG[:, l, 1, n, wh:wh + 1], in1=xt[:, n, :T],
                        op0=ALU.mult, op1=ALU.add), reads=[B_PS[pb], B_x, B_MOD], writes=[B_x])
                p.dma("pool", xT[:, :, t0:t0 + T].rearrange("c p t -> p c t"), xt[:, :, :T], sx, reads=[B_x], writes=[B_xT[ti]])
            p.barrier()

    phase_ada()
    for l in range(L):
        last = last_flags[l]
        phase_ffn(l, 0, 0, list(range(NT)), False)
        phase_inproj1(l)
        phase_inproj2(l)
        phase_inproj3(l)
        phase_conv(l)
        phase_ssd(l, last)
        phase_attn(l, last)
        phase_merge(l, last)
        fin = (l == L - 1)
        phase_ffn(l, 1, 2, list(range(1, NT)) if last else list(range(NT)), fin)
    p.finalize()
    return nc, p


def rope_tables(S):
    GRID_W = 64
    t = np.arange(S)
    row = (t // GRID_W).astype(np.float32)
    col = (t % GRID_W).astype(np.float32)
    inv = (10000.0 ** (-(np.arange(0, 64, 2, dtype=np.float32) / 64.0))).astype(np.float32)
    ar = (row[None, :] * inv[:, None]).astype(np.float32)
    ac = (col[None, :] * inv[:, None]).astype(np.float32)
    cos = np.concatenate([np.cos(ar), np.cos(ar), np.cos(ac), np.cos(ac)], 0)
    sin = np.concatenate([-np.sin(ar), np.sin(ar), -np.sin(ac), np.sin(ac)], 0)
    cosT = np.concatenate([np.ones((128, CTX), np.float32), cos.astype(np.float32)], 1)
    sinT = np.concatenate([np.zeros((128, CTX), np.float32), sin.astype(np.float32)], 1)
    return np.ascontiguousarray(cosT), np.ascontiguousarray(sinT)


def fm(v):
    sh = v.shape
    n = sh[-1] // 128
    r = v.reshape(*sh[:-1], n, 128)
    return np.ascontiguousarray(np.moveaxis(r, -1, 0))


def rep(v):
    return np.ascontiguousarray(np.broadcast_to(v[None], (128,) + v.shape))


def host_prep(inp, S, L):
    f = np.float32
    shared = {}
    shared["bada"] = fm(inp["b_ada"][:L].astype(f))
    shared["ng"] = fm(inp["norm_g"][:L].astype(f))
    shared["w_ada"] = np.ascontiguousarray(inp["w_ada"][:L])
    for k in ("ffn1_w13", "ffn1_w2", "ffn2_w13", "ffn2_w2", "w_in", "w_ssd_out", "w_attn_out", "w_out"):
        shared[k] = np.ascontiguousarray(inp[k][:L])
    shared["convw"] = fm(inp["conv_w"][:L].astype(f))
    shared["convb"] = fm(inp["conv_b"][:L].astype(f))
    shared["alog"] = rep(inp["a_log"][:L].reshape(L, 64).astype(f))
    shared["dtb"] = rep(inp["dt_bias"][:L].reshape(L, 64).astype(f))
    shared["dskip"] = rep(inp["d_skip"][:L].astype(f))
    shared["sng"] = rep(inp["ssd_norm_g"][:L].astype(f))
    g = inp["qk_norm_g"][:L].astype(f)
    gp = g.reshape(L, 2, 2, 2, 32)[:, :, :, ::-1, :].reshape(L, 2, 128)
    shared["qkg"] = np.ascontiguousarray(np.concatenate([g, gp], 1).transpose(2, 0, 1))
    cosT, sinT = rope_tables(S)
    shared["cosT"] = cosT
    shared["sinT"] = sinT
    shared["ident"] = np.eye(128, dtype=f)
    k = np.arange(128)[:, None]
    j = np.arange(128)[None, :]
    shared["masks"] = np.ascontiguousarray(np.stack([(k <= j), (k >= j), (k > j), (k < j)], 1).astype(f))
    ni = np.zeros((12, 512), f)
    for jj in range(4):
        ni[jj * 3:(jj + 1) * 3, jj * 128:(jj + 1) * 128] = -1.0
    shared["negind"] = ni
    NEG = -30000.0
    nm = np.stack([np.tile((k > j).astype(f) * NEG, (1, 4)), np.tile((k < j).astype(f) * NEG, (1, 4))], 1)
    shared["negmask"] = np.ascontiguousarray(nm.astype(f))
    maps = []
    B = inp["x"].shape[0]
    for b in range(B):
        m = dict(shared)
        xa = np.concatenate([inp["ctx"][b], inp["x"][b][:S]], 0).astype(f)
        m["xT0"] = np.ascontiguousarray(xa.T.reshape(KC, 128, CTX + S))
        cc = np.stack([inp["c"][b], inp["c_ctx"]], -1).astype(f)
        m["csT"] = np.ascontiguousarray(cc.reshape(KC, 128, 2).transpose(1, 0, 2))
        maps.append(m)
    return maps


_CACHE = {}


def kernel(**inputs):
    S, L = 4096, 4
    inp = {k: np.asarray(v) for k, v in inputs.items()}
    if "nc" not in _CACHE:
        _CACHE["nc"] = build_program(S, L)[0]
    nc = _CACHE["nc"]
    maps = host_prep(inp, S, L)
    res = run_bass_kernel_spmd(nc, maps, core_ids=list(range(8)))
    outs = []
    for r in res.results:
        o = np.asarray(r["outT"])
        outs.append(np.ascontiguousarray(o.reshape(D, S).T))
    return np.stack(outs, 0).astype(np.float32)
```
